# Optimizing a Trainium2 kernel written in Bass

```python
import jax, jax.numpy as jnp
from jax import lax
import numpy as np

D_MODEL = 1024
BATCH = 16
SEQ = 2048
DEPTH = 4

CONV_WIDTH = D_MODEL
CONV_K = 3
SGU_WIDTH = D_MODEL
SGU_CHUNK = 128
SGU_HEADS = 8
SGU_HEAD_DIM = SGU_WIDTH // SGU_HEADS
LRU_WIDTH = D_MODEL
LRU_HEADS = 16
LRU_HEAD_DIM = LRU_WIDTH // LRU_HEADS
LRU_CONV_K = 4
LRU_C = 8.0
N_BRANCH = 3
IN_COLS = 4 * CONV_WIDTH + 3 * SGU_WIDTH + 2 * LRU_WIDTH + N_BRANCH * D_MODEL
EPS = 1e-6

kernel_name = "hybrid_gated_conv_sgu_rglru"


def rms_norm(x, g):
    xf = x.astype(jnp.float32)
    y = xf * lax.rsqrt(jnp.mean(xf * xf, axis=-1, keepdims=True) + EPS)
    return (y * g.astype(jnp.float32)).astype(x.dtype)


def causal_dwconv(u, w):
    k = w.shape[0]
    s = u.shape[1]
    up = jnp.pad(u, ((0, 0), (k - 1, 0), (0, 0)))
    return sum(w[j] * up[:, j:j + s] for j in range(k))


def short_conv_mixer(x_a, b_gate, c_gate, w_conv):
    return b_gate * causal_dwconv(c_gate * x_a, w_conv)


def sgu_mixer(u, v, w_s, b_s):
    bsz, s, _ = v.shape
    nc = s // SGU_CHUNK
    vh = v.reshape(bsz, nc, SGU_CHUNK, SGU_HEADS, SGU_HEAD_DIM)
    vf = vh.astype(jnp.float32)
    mu = jnp.mean(vf, axis=-1, keepdims=True)
    var = jnp.mean(jnp.square(vf - mu), axis=-1, keepdims=True)
    vn = ((vf - mu) * lax.rsqrt(var + EPS)).astype(v.dtype)
    causal = jnp.tril(jnp.ones((SGU_CHUNK, SGU_CHUNK), dtype=bool))
    ws = jnp.where(causal[None], w_s, jnp.zeros_like(w_s))
    z = jnp.einsum("hts,bnshd->bnthd", ws, vn) + b_s.T[None, None, :, :, None]
    return u * z.reshape(bsz, s, SGU_WIDTH)


def rg_lru_mixer(x_r, w_conv, b_conv, w_a, b_a, w_x, b_x, lam):
    bsz, s, _ = x_r.shape
    xc = causal_dwconv(x_r, w_conv) + b_conv
    xh = xc.reshape(bsz, s, LRU_HEADS, LRU_HEAD_DIM)
    r = jax.nn.sigmoid(jnp.einsum("bshd,hde->bshe", xh, w_a) + b_a).reshape(bsz, s, LRU_WIDTH)
    i = jax.nn.sigmoid(jnp.einsum("bshd,hde->bshe", xh, w_x) + b_x).reshape(bsz, s, LRU_WIDTH)
    log_a = -LRU_C * r.astype(jnp.float32) * jax.nn.softplus(-lam.astype(jnp.float32))
    a = jnp.exp(log_a)
    mult = jnp.sqrt(-jnp.expm1(2.0 * log_a))
    b = mult * (i * xc).astype(jnp.float32)

    def combine(left, right):
        a_l, b_l = left
        a_r, b_r = right
        return a_l * a_r, a_r * b_l + b_r

    _, h = lax.associative_scan(combine, (a, b), axis=1)
    return h.astype(x_r.dtype)


def hybrid_layer(x, c_act, gain, w_mod, b_mod, w_in, w_out, conv_a_w, sgu_w, sgu_b,
                 lru_conv_w, lru_conv_b, lru_wa, lru_ba, lru_wx, lru_bx, lru_lambda):
    mod = c_act @ w_mod + b_mod
    shift, scale, gate = jnp.split(mod, 3, axis=-1)
    h = rms_norm(x, gain) * (1.0 + scale[:, None, :]) + shift[:, None, :]
    proj = h @ w_in
    sizes = (CONV_WIDTH,) * 4 + (SGU_WIDTH,) * 3 + (LRU_WIDTH,) * 2 + (D_MODEL,) * N_BRANCH
    splits = np.cumsum(sizes)[:-1].tolist()
    (a_x, a_b, a_c, a_z, s_u, s_v, s_z, r_x, r_z, g_a, g_s, g_r) = jnp.split(proj, splits, axis=-1)
    y_a = jax.nn.silu(a_z) * short_conv_mixer(a_x, a_b, a_c, conv_a_w)
    y_s = jax.nn.silu(s_z) * sgu_mixer(s_u, s_v, sgu_w, sgu_b)
    y_r = jax.nn.silu(r_z) * rg_lru_mixer(r_x, lru_conv_w, lru_conv_b, lru_wa, lru_ba,
                                          lru_wx, lru_bx, lru_lambda)
    merged = jax.nn.sigmoid(g_a) * y_a + jax.nn.sigmoid(g_s) * y_s + jax.nn.sigmoid(g_r) * y_r
    return x + gate[:, None, :] * (merged @ w_out)


def setup_inputs(seed: int = 0) -> dict:
    key = jax.random.key(seed)
    ks = jax.random.split(key, 20)
    nrm = jax.random.normal
    d = D_MODEL
    x = nrm(ks[0], (BATCH, SEQ, d), jnp.float32)
    c = nrm(ks[1], (BATCH, d), jnp.float32)
    norm_gain = 1.0 + 0.05 * nrm(ks[2], (DEPTH, d), jnp.float32)
    w_mod = 0.1 * d ** -0.5 * nrm(ks[3], (DEPTH, d, 3 * d), jnp.float32)
    b_mod = 0.02 * nrm(ks[4], (DEPTH, 3 * d), jnp.float32)
    w_in = d ** -0.5 * nrm(ks[5], (DEPTH, d, IN_COLS), jnp.float32)
    w_out = d ** -0.5 * nrm(ks[6], (DEPTH, d, d), jnp.float32)
    conv_a_w = CONV_K ** -0.5 * nrm(ks[7], (DEPTH, CONV_K, CONV_WIDTH), jnp.float32)
    sgu_w = SGU_CHUNK ** -0.5 * nrm(ks[8], (DEPTH, SGU_HEADS, SGU_CHUNK, SGU_CHUNK), jnp.float32)
    sgu_b = 1.0 + 0.1 * nrm(ks[9], (DEPTH, SGU_HEADS, SGU_CHUNK), jnp.float32)
    lru_conv_w = LRU_CONV_K ** -0.5 * nrm(ks[10], (DEPTH, LRU_CONV_K, LRU_WIDTH), jnp.float32)
    lru_conv_b = 0.02 * nrm(ks[11], (DEPTH, LRU_WIDTH), jnp.float32)
    lru_wa = LRU_HEAD_DIM ** -0.5 * nrm(ks[12], (DEPTH, LRU_HEADS, LRU_HEAD_DIM, LRU_HEAD_DIM), jnp.float32)
    lru_ba = 0.02 * nrm(ks[13], (DEPTH, LRU_HEADS, LRU_HEAD_DIM), jnp.float32)
    lru_wx = LRU_HEAD_DIM ** -0.5 * nrm(ks[14], (DEPTH, LRU_HEADS, LRU_HEAD_DIM, LRU_HEAD_DIM), jnp.float32)
    lru_bx = 0.02 * nrm(ks[15], (DEPTH, LRU_HEADS, LRU_HEAD_DIM), jnp.float32)
    a_c = jax.random.uniform(ks[16], (DEPTH, LRU_WIDTH), jnp.float32, 0.9, 0.999)
    a_base = a_c ** (1.0 / LRU_C)
    lru_lambda = jnp.log(a_base) - jnp.log1p(-a_base)
    final_gain = 1.0 + 0.05 * nrm(ks[17], (d,), jnp.float32)
    return {"x": x, "c": c, "norm_gain": norm_gain, "w_mod": w_mod, "b_mod": b_mod,
            "w_in": w_in, "w_out": w_out, "conv_a_w": conv_a_w, "sgu_w": sgu_w, "sgu_b": sgu_b,
            "lru_conv_w": lru_conv_w, "lru_conv_b": lru_conv_b, "lru_wa": lru_wa, "lru_ba": lru_ba,
            "lru_wx": lru_wx, "lru_bx": lru_bx, "lru_lambda": lru_lambda, "final_gain": final_gain}


def reference(x, c, norm_gain, w_mod, b_mod, w_in, w_out, conv_a_w, sgu_w, sgu_b,
              lru_conv_w, lru_conv_b, lru_wa, lru_ba, lru_wx, lru_bx, lru_lambda, final_gain):
    c_act = jax.nn.silu(c)
    for l in range(DEPTH):
        x = hybrid_layer(x, c_act, norm_gain[l], w_mod[l], b_mod[l], w_in[l], w_out[l],
                         conv_a_w[l], sgu_w[l], sgu_b[l], lru_conv_w[l], lru_conv_b[l],
                         lru_wa[l], lru_ba[l], lru_wx[l], lru_bx[l], lru_lambda[l])
    return rms_norm(x, final_gain)
```

```python
import numpy as np
from contextlib import ExitStack
import concourse.bass as bass
import concourse.mybir as mybir
from concourse.bass_utils import run_bass_kernel_spmd

F32 = mybir.dt.float32
BF16 = mybir.dt.bfloat16
AF = mybir.ActivationFunctionType
ALU = mybir.AluOpType

D = 1024
KC = 8
TT = 512
EPS = 1e-6
PL = 120
CL = 32
NW = 18
NSTG = 2
NTMP = 12
B_AX, B_AB, B_AC, B_AZ, B_SU, B_SV, B_SZ, B_RX, B_RZ, B_GA, B_GS, B_GR = range(12)


class DSem:
    def __init__(self, h):
        self.h = h
        self.val = 0


class Buf:
    __slots__ = ("name", "w", "r", "excl")

    def __init__(self, name, excl=False):
        self.name = name
        self.w = None
        self.r = {}
        self.excl = excl


class Sched:
    ENGS = ("pe", "act", "dve", "pool", "sp")

    def __init__(self):
        self.q = {e: [] for e in self.ENGS}
        self.cnt = {e: 0 for e in self.ENGS}
        self.seen = {e: {} for e in self.ENGS}

    def op(self, eng, fn, reads=(), writes=(), dma=None):
        deps = {}

        def add(k, v):
            if eng == "pe" and k == "pe":
                return
            if deps.get(k, 0) < v:
                deps[k] = v

        for b in reads:
            if b.w is not None:
                add(*b.w)
            if b.excl:
                for k, v in b.r.items():
                    if k != eng:
                        add(k, v)
        for b in writes:
            if b.w is not None:
                add(*b.w)
            for k, v in b.r.items():
                add(k, v)
        waits = []
        seen = self.seen[eng]
        for k, v in deps.items():
            if seen.get(k, 0) < v:
                seen[k] = v
                waits.append((k, v))
        if dma is None:
            self.cnt[eng] += 1
            me = (eng, self.cnt[eng])
        else:
            dma.val += 16
            me = (dma, dma.val)
        self.q[eng].append((waits, fn, me))
        for b in reads:
            if b.r.get(me[0], 0) < me[1]:
                b.r[me[0]] = me[1]
        for b in writes:
            b.w = me
            b.r = {}
        return me


def build(NSEQ=2, S=2048, L=4):
    NT = S // TT
    NCH = S // 128
    NSMALL = L * PL + 8 + 8 * NSEQ
    nc = bass.Bass("TRN2", target_bir_lowering=False, dynamic_dma_scratch_size=2048)

    def dram(name, shape, kind="ExternalInput"):
        return nc.dram_tensor(name, shape, F32, kind=kind).ap()

    xT_d = dram("xT", [NSEQ, 128, 8, S])
    small_d = dram("small", [128, NSMALL])
    win_d = dram("win", [L, 96, 128, 8, 128])
    wout_d = dram("wout", [L, 8, 128, 8, 128])
    wmod_d = dram("wmod", [L, 24, 128, 8, 128])
    wg_d = dram("wg", [L, 128, 8, 128])
    wst_d = dram("wst", [L, 128, 8, 128])
    sgub_d = dram("sgub", [L, 1, 8, 128])
    out_d = dram("outT", [NSEQ, 128, 8, S], kind="ExternalOutput")

    sc = Sched()
    es = ExitStack()
    with es:
        def sb(name, shape, dt):
            return es.enter_context(nc.sbuf_tensor(name, shape, dt))

        def mksem(name):
            return DSem(es.enter_context(nc.semaphore(name)))

        esem = {e: es.enter_context(nc.semaphore("s_" + e)) for e in ("pe", "act", "dve", "pool")}

        xT = sb("xT_sb", [128, 8, S], F32)
        hT = sb("hT_sb", [128, 8, S], BF16)
        mT = sb("mT_sb", [128, 8, S], BF16)
        ring = sb("ring", [128, NW, 8, 128], BF16)
        stage = sb("stage", [128, NSTG, 8, 128], F32)
        tmps = sb("tmps", [128, NTMP, TT], F32)
        cv = sb("cv", [128, 3 + S], BF16)
        xcb = sb("xcb", [128, 2, TT], BF16)
        vn = sb("vn", [128, 1, NCH, 128], BF16)
        gbf = sb("gbf", [128, 8, 128], BF16)
        wstb = sb("wstb", [128, 8, 128], BF16)
        brow = sb("brow", [128, 8, 128], BF16)
        small = sb("small_sb", [128, NSMALL], F32)
        coef = sb("coef", [128, L * CL], F32)
        coefAG = sb("coefAG", [128, 2, 16], F32)
        modT = sb("modT", [128, L, 24, NSEQ], F32)
        cact = sb("cact", [128, 8, NSEQ], BF16)
        ident = sb("ident", [128, 128], F32)
        ones_bf = sb("ones_bf", [128, 128], BF16)
        onesrow = sb("onesrow", [128, 128], BF16)
        diagA = sb("diagA", [128, 2, 3, 128], BF16)
        diagR = sb("diagR", [128, 2, 4, 128], BF16)
        bd = sb("bd", [128, 2, 2, 128], BF16)
        stats = sb("stats", [128, NCH, 6], F32)
        mv = sb("mv", [128, NCH, 2], F32)
        lnr = sb("lnr", [128, 2, NCH], F32)
        ltmp = sb("ltmp", [128, 8], F32)
        banks = [es.enter_context(nc.psum_tensor("ps%d" % i, [128, TT], F32)) for i in range(8)]

        xb = [Buf("x%d" % t) for t in range(NT)]
        hb = [Buf("h%d" % t) for t in range(NT)]
        mb = [[Buf("m%d_%d" % (k, t)) for t in range(NT)] for k in range(KC)]
        ringb = [Buf("ring%d" % i) for i in range(NW)]
        stageb = [Buf("stage%d" % i) for i in range(NSTG)]
        tmpb = [Buf("tmp%d" % i) for i in range(NTMP)]
        cvb = [Buf("cv%d" % t) for t in range(NT)]
        cvhalo = Buf("cvhalo")
        xcbb = [Buf("xcb0"), Buf("xcb1")]
        vnb = [Buf("vn0"), Buf("vn0b")]
        vnb[1] = vnb[0]
        gbfb, wstbb, browb, smallb = Buf("gbf"), Buf("wstb"), Buf("brow"), Buf("small")
        coefb, modb, cactb, constb = Buf("coef"), Buf("mod"), Buf("cact"), Buf("const")
        coefAGb = [Buf("cag0"), Buf("cag1")]
        diagAb = [Buf("dA0"), Buf("dA1")]
        diagRb = [Buf("dR0"), Buf("dR1")]
        bdb = [Buf("bd0"), Buf("bd1")]
        statb = Buf("stats")
        bankb = [Buf("bank%d" % i, excl=True) for i in range(8)]

        stage_sem = [mksem("stg%d" % i) for i in range(NSTG)]
        x_sem = [mksem("xs%d" % t) for t in range(NT)]
        NOUT = max(1, (8 * S // 2) // (8 * TT))
        out_sem = [mksem("os%d" % i) for i in range(2 * NT)]
        outb = [Buf("outst%d" % i) for i in range(NOUT)]
        small_sem = mksem("smalls")
        ring_sem = [mksem("rg%d" % i) for i in range(NW)]
        outst = mT[:].bitcast(F32)

        st = {"bank": 0, "chunk": 0}
        free_tmps = list(range(NTMP))

        def talloc():
            assert free_tmps, "out of temporaries"
            i = free_tmps.pop(0)
            return i

        def tfree(i):
            free_tmps.append(i)

        def T(i):
            return tmps[:, i, :]

        def nbank():
            i = st["bank"]
            st["bank"] = (i + 1) % 8
            return i

        def col(c):
            return small[:, c:c + 1]

        def tsl(t):
            return slice(t * TT, (t + 1) * TT)

        def act(out, in_, func, reads, writes, bias=None, scale=None):
            kw = {}
            if bias is not None:
                kw["bias"] = bias
            if scale is not None:
                kw["scale"] = scale
            sc.op("act", lambda e: e.activation(out, in_, func, **kw), reads, writes)

        def tt(eng, out, a, b, op, reads, writes):
            sc.op(eng, lambda e: e.tensor_tensor(out, a, b, op), reads, writes)

        def stt(out, in0, scalar, in1, op0, op1, reads, writes):
            sc.op("dve", lambda e: e.scalar_tensor_tensor(out, in0, scalar, in1, op0, op1), reads, writes)

        def tsc(eng, out, in0, s1, s2, op0, op1, reads, writes):
            if s2 is None:
                sc.op(eng, lambda e: e.tensor_scalar(out, in0, s1, None, op0), reads, writes)
            else:
                sc.op(eng, lambda e: e.tensor_scalar(out, in0, s1, s2, op0, op1), reads, writes)

        def mm(out, lhsT, rhs, start, stop, reads, writes):
            sc.op("pe", lambda e: e.matmul(out, lhsT, rhs, start=start, stop=stop), reads, writes)

        def load_chunk(kind, src, extra=None):
            i = st["chunk"]
            st["chunk"] += 1
            s = i % NSTG
            if kind == "sgub":
                sc.op("sp", lambda e: e.dma_start(out=stage[0:1, s], in_=src), (), [stageb[s]], dma=stage_sem[s])
                sc.op("sp", lambda e: e.dma_start(out=stage[32:33, s], in_=src), (), [stageb[s]], dma=stage_sem[s])
            else:
                sc.op("sp", lambda e: e.dma_start(out=stage[:, s], in_=src), (), [stageb[s]], dma=stage_sem[s])
            return s

        ring_pos = {"i": 0}

        def load_weight(src):
            r = ring_pos["i"] % NW
            ring_pos["i"] += 1
            sc.op("pool", lambda e: e.dma_start(out=ring[:, r], in_=src), (), [ringb[r]], dma=ring_sem[r])
            return r

        sc.op("sp", lambda e: e.dma_start(out=small[:], in_=small_d), (), [smallb], dma=small_sem)
        sc.op("pool", lambda e: e.memset(ident[:], 1.0), (), [constb])
        sc.op("pool", lambda e: e.affine_select(ident[:], ident[:], [[1, 128]], ALU.is_equal, 0.0,
                                                base=0, channel_multiplier=-1), [constb], [constb])
        sc.op("pool", lambda e: e.memset(ones_bf[:], 1.0), (), [constb])
        sc.op("pool", lambda e: e.memset(onesrow[:], 0.0), (), [constb])
        sc.op("pool", lambda e: e.memset(onesrow[0:1, :], 1.0), [constb], [constb])
        sc.op("pool", lambda e: e.memset(onesrow[32:33, :], 1.0), [constb], [constb])
        sc.op("pool", lambda e: e.memset(brow[:], 0.0), (), [browb])
        sc.op("pool", lambda e: e.memset(bd[:], 0.0), (), [bdb[0], bdb[1]])
        sc.op("pool", lambda e: e.memset(cv[:, 0:3], 0.0), (), [cvhalo])
        cbase = L * PL + 8
        act(cact[:], small[:, cbase:cbase + 8 * NSEQ].rearrange("p (k b) -> p k b", b=NSEQ), AF.Silu,
            [smallb], [cactb])
        for l in range(L):
            b0 = l * PL
            c0 = l * CL
            tsc("dve", coef[:, c0:c0 + 8], small[:, b0 + 96:b0 + 104], 0.5, None, ALU.mult, None, [smallb], [coefb])
            tsc("dve", coef[:, c0 + 8:c0 + 16], small[:, b0 + 104:b0 + 112], 0.5, None, ALU.mult, None,
                [smallb], [coefb])
            tsc("dve", coef[:, c0 + 16:c0 + 24], small[:, b0 + 88:b0 + 96], 0.5, None, ALU.mult, None,
                [smallb], [coefb])
            act(ltmp[:], small[:, b0 + 112:b0 + 120], AF.Exp, [smallb], [statb], scale=-1.0)
            act(ltmp[:], ltmp[:], AF.Ln, [statb], [statb], bias=1.0)
            tsc("dve", coef[:, c0 + 24:c0 + 32], ltmp[:], -4.0, None, ALU.mult, None, [statb], [coefb])

        def grp_weights(kind, l, j):
            if kind == "A":
                blks = [B_AX, B_AC, B_AB, B_AZ, B_GA, B_SV]
            elif kind == "R":
                blks = [B_RX, B_RZ, B_GR]
            elif kind == "S":
                blks = [B_SU, B_SZ, B_GS]
            else:
                return [wout_d[l, j]]
            return [win_d[l, b * 8 + j] for b in blks]

        def main_mm(bk, r, t):
            for k in range(KC):
                mm(banks[bk][:], ring[:, r, k, :], hT[:, k, tsl(t)], k == 0, k == KC - 1,
                   [ringb[r], hb[t]], [bankb[bk]])

        def gate_pair(bz, bg, raw):
            t1, t2 = talloc(), talloc()
            act(T(t1), banks[bz][:], AF.Tanh, [bankb[bz]], [tmpb[t1]], scale=0.5)
            act(T(t2), banks[bg][:], AF.Tanh, [bankb[bg]], [tmpb[t2]], scale=0.5)
            stt(T(t1), T(t1), 1.0, banks[bz][:], ALU.add, ALU.mult, [tmpb[t1], bankb[bz]], [tmpb[t1]])
            stt(T(t1), T(t2), 1.0, T(t1), ALU.add, ALU.mult, [tmpb[t1], tmpb[t2]], [tmpb[t1]])
            tfree(t2)
            return t1

        def unit_stages(kind, l, j, t, slots, par):
            b0 = l * PL
            c0 = l * CL
            u = {}
            if kind == "A":
                def s0():
                    bx, bc, bb_, bz, bg = (nbank() for _ in range(5))
                    for bk, r in zip((bx, bc, bb_, bz, bg), slots):
                        main_mm(bk, r, t)
                    t1 = talloc()
                    act(T(t1), banks[bx][:], AF.Copy, [bankb[bx]], [tmpb[t1]])
                    tt("dve", cv[:, 3 + t * TT:3 + (t + 1) * TT], banks[bc][:], T(t1), ALU.mult,
                       [bankb[bc], tmpb[t1]], [cvb[t]])
                    tfree(t1)
                    q = gate_pair(bz, bg, None)
                    tt("dve", T(q), banks[bb_][:], T(q), ALU.mult, [bankb[bb_], tmpb[q]], [tmpb[q]])
                    u["q"] = q
                    rv = slots[5]
                    bvv = nbank()
                    for c4 in range(4):
                        c = t * 4 + c4
                        for k in range(KC):
                            mm(banks[bvv][:, c4 * 128:(c4 + 1) * 128], hT[:, k, c * 128:(c + 1) * 128],
                               ring[:, rv, k, :], k == 0, k == KC - 1, [hb[t], ringb[rv]], [bankb[bvv]])
                    tv = talloc()
                    act(T(tv), banks[bvv][:], AF.Copy, [bankb[bvv]], [tmpb[tv]])
                    for c4 in range(4):
                        c = t * 4 + c4
                        src = T(tv)[:, c4 * 128:(c4 + 1) * 128]
                        sc.op("dve", lambda e, c=c, src=src: e.bn_stats(stats[:, c, :], src), [tmpb[tv]], [statb])
                        sc.op("dve", lambda e, c=c: e.bn_aggr(mv[:, c, :], stats[:, c, :]), [statb], [statb])
                    c0_ = t * 4
                    act(lnr[:, 0, c0_:c0_ + 4], mv[:, c0_:c0_ + 4, 1], AF.Sqrt, [statb], [lnrb], bias=EPS)
                    sc.op("dve", lambda e: e.reciprocal(lnr[:, 0, c0_:c0_ + 4], lnr[:, 0, c0_:c0_ + 4]), [lnrb], [lnrb])
                    stt(lnr[:, 1, c0_:c0_ + 4], mv[:, c0_:c0_ + 4, 0], -1.0, lnr[:, 0, c0_:c0_ + 4], ALU.mult, ALU.mult,
                        [statb, lnrb], [lnrb])
                    for c4 in range(4):
                        c = c0_ + c4
                        tsc("pool", vn[:, 0, c, :], T(tv)[:, c4 * 128:(c4 + 1) * 128],
                            lnr[:, 0, c:c + 1], lnr[:, 1, c:c + 1], ALU.mult, ALU.add,
                            [tmpb[tv], lnrb], [vnb[par]])
                    tfree(tv)

                def s1():
                    bv = nbank()
                    rd = [diagAb[par], cvb[t], cvhalo] + ([cvb[t - 1]] if t > 0 else [])
                    for tap in range(3):
                        mm(banks[bv][:], diagA[:, par, tap, :], cv[:, 1 + t * TT + tap:1 + t * TT + tap + TT],
                           tap == 0, tap == 2, rd, [bankb[bv]])
                    q = u["q"]
                    tt("dve", mT[:, j, tsl(t)], banks[bv][:], T(q), ALU.mult, [bankb[bv], tmpb[q]], [mb[j][t]])
                    tfree(q)
                return [s0, s1]
            if kind == "R":
                def s0():
                    bx, bz, bg = (nbank() for _ in range(3))
                    for bk, r in zip((bx, bz, bg), slots):
                        main_mm(bk, r, t)
                    act(cv[:, 3 + t * TT:3 + (t + 1) * TT], banks[bx][:], AF.Copy, [bankb[bx]], [cvb[t]])
                    u["q"] = gate_pair(bz, bg, None)

                def s1():
                    bv = nbank()
                    rd = [diagRb[par], cvb[t], cvhalo] + ([cvb[t - 1]] if t > 0 else [])
                    for tap in range(4):
                        mm(banks[bv][:], diagR[:, par, tap, :], cv[:, t * TT + tap:t * TT + tap + TT],
                           tap == 0, tap == 3, rd, [bankb[bv]])
                    xp = t % 2
                    act(xcb[:, xp, :], banks[bv][:], AF.Identity, [bankb[bv], smallb], [xcbb[xp]],
                        bias=col(b0 + 88 + j))
                    t3 = talloc()
                    act(T(t3), banks[bv][:], AF.Identity, [bankb[bv], coefb], [tmpb[t3]],
                        bias=coef[:, c0 + 16 + j:c0 + 17 + j], scale=0.5)
                    u["xcf"] = t3

                def s2():
                    xp = t % 2
                    bra, bri = nbank(), nbank()
                    mm(banks[bra][:], bd[:, par, 0, :], xcb[:, xp, :], True, True, [bdb[par], xcbb[xp]], [bankb[bra]])
                    mm(banks[bri][:], bd[:, par, 1, :], xcb[:, xp, :], True, True, [bdb[par], xcbb[xp]], [bankb[bri]])
                    t4, t5, t6 = talloc(), talloc(), talloc()
                    act(T(t4), banks[bra][:], AF.Tanh, [bankb[bra], coefb], [tmpb[t4]],
                        bias=coef[:, c0 + j:c0 + j + 1], scale=0.5)
                    act(T(t5), banks[bri][:], AF.Tanh, [bankb[bri], coefb], [tmpb[t5]],
                        bias=coef[:, c0 + 8 + j:c0 + 9 + j], scale=0.5)
                    hc = coef[:, c0 + 24 + j:c0 + 25 + j]
                    act(T(t4), T(t4), AF.Exp, [tmpb[t4], coefb], [tmpb[t4]], bias=hc, scale=hc)
                    tt("pool", T(t6), T(t4), T(t4), ALU.mult, [tmpb[t4]], [tmpb[t6]])
                    tsc("pool", T(t6), T(t6), 0.9999999, 0.0, ALU.min, ALU.max, [tmpb[t6]], [tmpb[t6]])
                    t3 = u["xcf"]
                    stt(T(t5), T(t5), 1.0, T(t3), ALU.add, ALU.mult, [tmpb[t5], tmpb[t3]], [tmpb[t5]])
                    tfree(t3)
                    q = u["q"]

                    def s2b():
                        act(T(t6), T(t6), AF.Sqrt, [tmpb[t6]], [tmpb[t6]], bias=1.0, scale=-1.0)
                        tt("pool", T(t5), T(t5), T(t6), ALU.mult, [tmpb[t5], tmpb[t6]], [tmpb[t5]])
                        tfree(t6)
                        t7 = talloc()
                        prev = scan_prev.get("t")
                        if t == 0:
                            init = 0.0
                            rd = [tmpb[t4], tmpb[t5]]
                        else:
                            init = T(prev)[:, TT - 1:TT]
                            rd = [tmpb[t4], tmpb[t5], tmpb[prev]]
                        sc.op("dve", lambda e: e.tensor_tensor_scan(T(t7), T(t4), T(t5), init, ALU.mult, ALU.add),
                              rd, [tmpb[t7]])
                        if prev is not None:
                            tfree(prev)
                        tfree(t4)
                        tfree(t5)
                        tt("pool", T(q), T(t7), T(q), ALU.mult, [tmpb[t7], tmpb[q]], [tmpb[q]])
                        tt("dve", mT[:, j, tsl(t)], mT[:, j, tsl(t)], T(q), ALU.add, [mb[j][t], tmpb[q]], [mb[j][t]])
                        tfree(q)
                        if t == NT - 1:
                            tfree(t7)
                            scan_prev["t"] = None
                        else:
                            scan_prev["t"] = t7

                    r_stash.append(s2b)
                    if t % 2 == 1 or t == NT - 1:
                        while r_stash:
                            r_stash.pop(0)()
                return [s0, s1, s2]
            if kind == "S":
                def s0():
                    bu, bz, bg = (nbank() for _ in range(3))
                    for bk, r in zip((bu, bz, bg), slots):
                        main_mm(bk, r, t)
                    bzf = nbank()
                    for c4 in range(4):
                        c = t * 4 + c4
                        o = banks[bzf][:, c4 * 128:(c4 + 1) * 128]
                        mm(o, vn[:, 0, c, :], wstb[:, j, :], True, False, [vnb[par], wstbb], [bankb[bzf]])
                        mm(o, onesrow[:], brow[:, j, :], False, True, [constb, browb], [bankb[bzf]])
                    q = gate_pair(bz, bg, None)
                    tt("dve", T(q), banks[bu][:], T(q), ALU.mult, [bankb[bu], tmpb[q]], [tmpb[q]])
                    tt("dve", T(q), banks[bzf][:], T(q), ALU.mult, [bankb[bzf], tmpb[q]], [tmpb[q]])
                    tt("pool", mT[:, j, tsl(t)], mT[:, j, tsl(t)], T(q), ALU.add, [mb[j][t], tmpb[q]], [mb[j][t]])
                    tfree(q)
                return [s0]
            if kind == "V":
                def s0():
                    r = slots[0]
                    vbanks = []
                    for c in range(NCH):
                        if c % 4 == 0:
                            vbanks.append(nbank())
                        bk = vbanks[-1]
                        for k in range(KC):
                            mm(banks[bk][:, (c % 4) * 128:(c % 4 + 1) * 128], hT[:, k, c * 128:(c + 1) * 128],
                               ring[:, r, k, :], k == 0, k == KC - 1, [hb[c // 4], ringb[r]], [bankb[bk]])
                    for c in range(NCH):
                        bk = vbanks[c // 4]
                        src = banks[bk][:, (c % 4) * 128:(c % 4 + 1) * 128]
                        sc.op("dve", lambda e, c=c, src=src: e.bn_stats(stats[:, c, :], src), [bankb[bk]], [statb])
                        sc.op("dve", lambda e, c=c: e.bn_aggr(mv[:, c, :], stats[:, c, :]), [statb], [statb])
                    act(lnr[:, 0, :], mv[:, :, 1], AF.Sqrt, [statb], [statb], bias=EPS)
                    sc.op("dve", lambda e: e.reciprocal(lnr[:, 0, :], lnr[:, 0, :]), [statb], [statb])
                    stt(lnr[:, 1, :], mv[:, :, 0], -1.0, lnr[:, 0, :], ALU.mult, ALU.mult, [statb], [statb])
                    for c in range(NCH):
                        bk = vbanks[c // 4]
                        src = banks[bk][:, (c % 4) * 128:(c % 4 + 1) * 128]
                        act(vn[:, 0, c, :], src, AF.Identity, [bankb[bk], statb], [vnb[par]],
                            bias=lnr[:, 1, c:c + 1], scale=lnr[:, 0, c:c + 1])
                return [s0]
            if kind == "O":
                def s0():
                    gp = u_par["ag"]
                    for o in range(KC):
                        bk = nbank()
                        r = slots[o]
                        for k in range(KC):
                            mm(banks[bk][:], ring[:, r, k, :], mT[:, k, tsl(t)], k == 0, k == KC - 1,
                               [ringb[r], mb[k][t]], [bankb[bk]])
                        stt(xT[:, o, tsl(t)], banks[bk][:], coefAG[:, gp, 8 + o:9 + o], xT[:, o, tsl(t)],
                            ALU.mult, ALU.add, [bankb[bk], coefAGb[gp], xb[t]], [xb[t]])
                return [s0]
            raise ValueError(kind)

        scan_prev = {"t": None}
        r_stash = []
        vtmps = []
        lnrb = Buf("lnr")
        u_par = {"ag": 0}

        def group_setup(kind, l, j, par):
            b0 = l * PL
            if kind == "A":
                for tap in range(3):
                    tsc("pool", diagA[:, par, tap, :], ident[:], col(b0 + 32 + tap * 8 + j), 0.0, ALU.mult, ALU.add,
                        [constb, smallb], [diagAb[par]])
            elif kind == "R":
                for tap in range(4):
                    tsc("pool", diagR[:, par, tap, :], ident[:], col(b0 + 56 + tap * 8 + j), 0.0, ALU.mult, ALU.add,
                        [constb, smallb], [diagRb[par]])
                gv = gbf[:, j, :].rearrange("p (g e) -> p g e", g=2)
                sc.op("pool", lambda e: e.tensor_copy(bd[0:64, par, :, 0:64], gv[0:64]), [gbfb], [bdb[par]])
                sc.op("pool", lambda e: e.tensor_copy(bd[64:128, par, :, 64:128], gv[64:128]), [gbfb], [bdb[par]])

        def norm_tile(t, final, l, b, ag):
            sc.op("act", lambda e: e.activation(hT[:, :, tsl(t)], xT[:, :, tsl(t)], AF.Square), [xb[t]], [hb[t]])
            bk = nbank()
            for k in range(KC):
                mm(banks[bk][:], ones_bf[:], hT[:, k, tsl(t)], k == 0, k == KC - 1, [constb, hb[t]], [bankb[bk]])
            t1 = talloc()
            act(T(t1), banks[bk][:], AF.Sqrt, [bankb[bk]], [tmpb[t1]], bias=EPS, scale=1.0 / D)
            sc.op("dve", lambda e: e.reciprocal(banks[bk][:], T(t1)), [tmpb[t1]], [bankb[bk]])
            tfree(t1)
            if not final:
                for k in range(KC):
                    t2 = talloc()
                    tt("dve", T(t2), xT[:, k, tsl(t)], banks[bk][:], ALU.mult, [xb[t], bankb[bk]], [tmpb[t2]])
                    act(hT[:, k, tsl(t)], T(t2), AF.Identity, [tmpb[t2], coefAGb[ag], modb], [hb[t]],
                        bias=modT[:, l, k, b:b + 1], scale=coefAG[:, ag, k:k + 1])
                    tfree(t2)
            else:
                fg = L * PL
                H2 = TT // 2
                mview = mT[:, :, tsl(t)].bitcast(F32)
                hview = hT[:, :, tsl(t)].bitcast(F32)
                mbt = [mb[k_][t] for k_ in range(KC)]
                for k in range(KC):
                    stt(mview[:, k, :], xT[:, k, t * TT:t * TT + H2], col(fg + k), banks[bk][:, 0:H2],
                        ALU.mult, ALU.mult, [xb[t], smallb, bankb[bk]], mbt)
                    stt(hview[:, k, :], xT[:, k, t * TT + H2:(t + 1) * TT], col(fg + k), banks[bk][:, H2:TT],
                        ALU.mult, ALU.mult, [xb[t], smallb, bankb[bk]], [hb[t]])
                sc.op("sp", lambda e: e.dma_start(out=out_d[b, :, :, t * TT:t * TT + H2], in_=mview),
                      mbt, (), dma=out_sem[2 * t])
                sc.op("sp", lambda e: e.dma_start(out=out_d[b, :, :, t * TT + H2:(t + 1) * TT], in_=hview),
                      [hb[t]], (), dma=out_sem[2 * t + 1])

        groups = []
        for b in range(NSEQ):
            for l in range(L):
                if b == 0 and l == 0:
                    groups += [("M", 0, 0, q) for q in range(8)]
                for j in range(KC):
                    groups.append(("A", b, l, j))
                    groups.append(("RS", b, l, j))
                    if b == 0 and l + 1 < L:
                        groups.append(("M", 0, l + 1, j))
                groups.append(("O", b, l, 0))
        gslots = {}

        def nchunks(gi):
            if gi >= len(groups):
                return 0
            return {"A": 6, "RS": 6, "O": 8, "M": 3}[groups[gi][0]]

        def emit_loads(gi):
            if gi >= len(groups) or gi in gslots:
                return
            kind, b, l, j = groups[gi]
            if kind == "RS":
                srcs = grp_weights("R", l, j) + grp_weights("S", l, j)
            elif kind == "O":
                srcs = [wout_d[l, o] for o in range(KC)]
            elif kind == "M":
                srcs = [wmod_d[l, 3 * j + fi] for fi in range(3)]
            else:
                srcs = grp_weights(kind, l, j)
            gslots[gi] = [load_weight(src) for src in srcs]

        pending = []

        def pump(new_unit):
            if new_unit is not None:
                pending.insert(0, [new_unit, 0])
            for ent in list(pending):
                stages, i = ent
                stages[i]()
                ent[1] += 1
            pending[:] = [e for e in pending if e[1] < len(e[0])]

        def drain():
            while pending:
                pump(None)

        def layer_setup(b, l):
            ag = (b * L + l) % 2
            stt(coefAG[:, ag, 0:8], modT[:, l, 8:16, b], 1.0, small[:, l * PL:l * PL + 8], ALU.add, ALU.mult,
                [modb, smallb], [coefAGb[ag]])
            tsc("dve", coefAG[:, ag, 8:16], modT[:, l, 16:24, b], 0.25, None, ALU.mult, None,
                [modb], [coefAGb[ag]])
            s = load_chunk("wst", wst_d[l])
            sc.op("pool", lambda e, s=s: e.affine_select(wstb[:], stage[:, s], [[0, 8], [1, 128]], ALU.is_ge, 0.0,
                                                         base=0, channel_multiplier=-1), [stageb[s]], [wstbb])
            s = load_chunk("sgub", sgub_d[l])
            sc.op("dve", lambda e, s=s: e.tensor_copy(brow[0:1], stage[0:1, s]), [stageb[s]], [browb])
            sc.op("dve", lambda e, s=s: e.tensor_copy(brow[32:33], stage[32:33, s]), [stageb[s]], [browb])
            sc.op("dve", lambda e, s=s: e.tensor_tensor(brow[32:33], stage[32:33, s], brow[32:33], ALU.subtract),
                  [stageb[s], browb], [browb])
            s = load_chunk("wg", wg_d[l])
            sc.op("pool", lambda e, s=s: e.tensor_copy(gbf[:], stage[:, s]), [stageb[s]], [gbfb])
            return ag

        def prefetch(gi):
            emit_loads(gi)
            emit_loads(gi + 1)
            if nchunks(gi) + nchunks(gi + 1) + nchunks(gi + 2) <= NW:
                emit_loads(gi + 2)


        def run_M(lt, q, slots):
            bk = nbank()
            for fi in range(3):
                for k in range(KC):
                    mm(banks[bk][:, fi * NSEQ:(fi + 1) * NSEQ], ring[:, slots[fi], k, :], cact[:, k, :],
                       k == 0, k == KC - 1, [ringb[slots[fi]], cactb], [bankb[bk]])
            pv = banks[bk][:, 0:3 * NSEQ].rearrange("p (f b) -> p f b", b=NSEQ)
            c0 = lt * PL + 8 + 3 * q
            for b2 in range(NSEQ):
                tt("dve", modT[:, lt, 3 * q:3 * q + 3, b2], pv[:, :, b2], small[:, c0:c0 + 3], ALU.add,
                   [bankb[bk], smallb], [modb])

        def xload(b, t):
            sc.op("sp", lambda e: e.dma_start(out=xT[:, :, tsl(t)], in_=xT_d[b, :, :, tsl(t)]),
                  (), [xb[t]], dma=x_sem[t])

        gi = 0
        emit_loads(0)
        ag = None
        start_norms = []
        deferred = []
        for b in range(NSEQ):
            if b == 0:
                for t in range(NT):
                    xload(0, t)
                while groups[gi][0] == "M":
                    prefetch(gi)
                    run_M(groups[gi][2], groups[gi][3], gslots.pop(gi))
                    gi += 1
                ag = layer_setup(0, 0)
                for t in range(min(2, NT)):
                    norm_tile(t, False, 0, 0, ag)
                start_norms = list(range(min(2, NT), NT))
            for l in range(L):
                while groups[gi][0] != "O":
                    kind, gb, gl, j = groups[gi]
                    prefetch(gi)
                    slots = gslots.pop(gi)
                    par = j % 2
                    if kind == "M":
                        run_M(gl, j, slots)
                    elif kind == "A":
                        assert (gb, gl) == (b, l)
                        group_setup("A", l, j, par)
                        group_setup("R", l, j, par)
                        for t in range(NT):
                            pump(unit_stages("A", l, j, t, slots, par))
                            if start_norms:
                                norm_tile(start_norms.pop(0), False, 0, 0, ag)
                            if deferred:
                                deferred.pop(0)()
                    else:
                        assert (gb, gl) == (b, l)
                        for t in range(NT):
                            pump(unit_stages("R", l, j, t, slots[:3], par))
                            pump(unit_stages("S", l, j, t, slots[3:], par))
                    gi += 1
                prefetch(gi)
                slots = gslots.pop(gi)
                u_par["ag"] = ag
                last = l + 1 == L
                if not last:
                    nag = layer_setup(b, l + 1)
                elif b + 1 < NSEQ:
                    nag = layer_setup(b + 1, 0)
                else:
                    nag = None
                def tail1(t, last=last, l=l, b=b, nag=nag):
                    if not last:
                        norm_tile(t, False, l + 1, b, nag)
                    else:
                        norm_tile(t, True, L, b, 0)
                        if b + 1 < NSEQ:
                            xload(b + 1, t)

                def tail2(t, last=last, b=b, nag=nag):
                    if last and b + 1 < NSEQ:
                        norm_tile(t, False, 0, b + 1, nag)

                for t in range(NT):
                    pump(unit_stages("O", l, 0, t, slots, 0))
                    if t == 0:
                        drain()
                    if t >= 1:
                        tail1(t - 1)
                    if t >= 2:
                        tail2(t - 2)
                is_end = last and b + 1 == NSEQ
                if is_end:
                    tail1(NT - 1)
                else:
                    deferred.append(lambda t1=tail1: t1(NT - 1))
                    if NT >= 2:
                        deferred.append(lambda t2=tail2: t2(NT - 2))
                    deferred.append(lambda t2=tail2: t2(NT - 1))
                gi += 1
                ag = nag
        sc.q["sp"].append(([(s_, s_.val) for s_ in out_sem if s_.val > 0], None, None))

        def semh(k):
            return k.h if isinstance(k, DSem) else esem[k]

        def replay(name, e):
            for waits, fn, me in sc.q[name]:
                for k, v in waits:
                    e.wait_ge(semh(k), v)
                if fn is None:
                    continue
                ins = fn(e)
                if isinstance(me[0], DSem):
                    ins.then_inc(me[0].h, 16)
                else:
                    ins.then_inc(esem[me[0]], 1)

        with nc.Block() as block:
            @block.tensor
            def _(e):
                replay("pe", e)

            @block.scalar
            def _(e):
                replay("act", e)

            @block.vector
            def _(e):
                replay("dve", e)

            @block.gpsimd
            def _(e):
                replay("pool", e)

            @block.sync
            def _(e):
                replay("sp", e)
    return nc


def _chunk_cols(w):
    L, K, C = w.shape
    return np.ascontiguousarray(w.reshape(L, 8, 128, C // 128, 128).transpose(0, 3, 2, 1, 4))


def _feat(v):
    sh = v.shape[:-1]
    a = v.reshape(sh + (8, 128))
    return np.moveaxis(a, -1, 0)


def prep_shared(norm_gain, w_mod, b_mod, w_in, w_out, conv_a_w, sgu_w, sgu_b, lru_conv_w, lru_conv_b,
                lru_wa, lru_ba, lru_wx, lru_bx, lru_lambda, final_gain):
    L = w_in.shape[0]
    f32 = np.float32
    sm = np.zeros((128, L, PL), f32)
    sm[:, :, 0:8] = _feat(norm_gain)
    sm[:, :, 8:32] = np.moveaxis(b_mod.reshape(L, 24, 128), -1, 0)
    sm[:, :, 32:56] = _feat(conv_a_w).reshape(128, L, 24)
    sm[:, :, 56:88] = _feat(lru_conv_w).reshape(128, L, 32)
    sm[:, :, 88:96] = _feat(lru_conv_b)
    sm[:, :, 96:104] = _feat(lru_ba.reshape(L, 1024))
    sm[:, :, 104:112] = _feat(lru_bx.reshape(L, 1024))
    sm[:, :, 112:120] = _feat(lru_lambda)
    shared = {
        "small_layers": sm.reshape(128, L * PL),
        "fgain": np.ascontiguousarray(_feat(final_gain)),
        "win": _chunk_cols(w_in),
        "wout": _chunk_cols(w_out),
        "wmod": _chunk_cols(w_mod),
        "wg": np.ascontiguousarray(
            np.stack([lru_wa, lru_wx], axis=0).reshape(2, L, 8, 2, 64, 64).transpose(1, 3, 4, 2, 0, 5)
        ).reshape(L, 128, 8, 128),
        "wst": np.ascontiguousarray(sgu_w.transpose(0, 3, 1, 2)),
        "sgub": np.ascontiguousarray(sgu_b.reshape(L, 1, 8, 128)),
    }
    return shared


def make_in_maps(x, c, shared, n_cores, NSEQ):
    in_maps = []
    B, S, _ = x.shape
    for i in range(n_cores):
        xs = x[i * NSEQ:(i + 1) * NSEQ]
        xT = np.ascontiguousarray(xs.reshape(NSEQ, S, 8, 128).transpose(0, 3, 2, 1))
        cs = c[i * NSEQ:(i + 1) * NSEQ]
        cT = cs.reshape(NSEQ, 8, 128).transpose(2, 1, 0).reshape(128, 8 * NSEQ)
        small = np.ascontiguousarray(
            np.concatenate([shared["small_layers"], shared["fgain"], cT], axis=1).astype(np.float32))
        m = {"xT": xT, "small": small}
        for k in ("win", "wout", "wmod", "wg", "wst", "sgub"):
            m[k] = shared[k]
        in_maps.append(m)
    return in_maps


_NC_CACHE = {}


def kernel(x, c, norm_gain, w_mod, b_mod, w_in, w_out, conv_a_w, sgu_w, sgu_b, lru_conv_w, lru_conv_b,
           lru_wa, lru_ba, lru_wx, lru_bx, lru_lambda, final_gain):
    args = [np.asarray(a, dtype=np.float32) for a in
            (norm_gain, w_mod, b_mod, w_in, w_out, conv_a_w, sgu_w, sgu_b, lru_conv_w, lru_conv_b,
             lru_wa, lru_ba, lru_wx, lru_bx, lru_lambda, final_gain)]
    x = np.asarray(x, dtype=np.float32)
    c = np.asarray(c, dtype=np.float32)
    B, S, _ = x.shape
    L = args[3].shape[0]
    n_cores = 8
    NSEQ = B // n_cores
    shared = prep_shared(*args)
    in_maps = make_in_maps(x, c, shared, n_cores, NSEQ)
    key = (NSEQ, S, L)
    if key not in _NC_CACHE:
        _NC_CACHE[key] = build(NSEQ, S, L)
    nc = _NC_CACHE[key]
    res = run_bass_kernel_spmd(nc, in_maps, core_ids=list(range(n_cores)))
    out = np.empty((B, S, D), np.float32)
    for i in range(n_cores):
        oT = np.asarray(res.results[i]["outT"]).reshape(NSEQ, 128, 8, S)
        out[i * NSEQ:(i + 1) * NSEQ] = oT.transpose(0, 3, 2, 1).reshape(NSEQ, S, D)
    return out
```

```python
import numpy as np
from contextlib import ExitStack
import concourse.bass as bass
import concourse.mybir as mybir
from concourse.bass_utils import run_bass_kernel_spmd

F32 = mybir.dt.float32
BF16 = mybir.dt.bfloat16
AF = mybir.ActivationFunctionType
ALU = mybir.AluOpType

D = 1024
KC = 8
TT = 512
EPS = 1e-6
PL = 120
CL = 32
NW = 18
NSTG = 2
NTMP = 12
B_AX, B_AB, B_AC, B_AZ, B_SU, B_SV, B_SZ, B_RX, B_RZ, B_GA, B_GS, B_GR = range(12)


class DSem:
    def __init__(self, h):
        self.h = h
        self.val = 0


class Buf:
    __slots__ = ("name", "w", "r", "excl")

    def __init__(self, name, excl=False):
        self.name = name
        self.w = None
        self.r = {}
        self.excl = excl


class Sched:
    ENGS = ("pe", "act", "dve", "pool", "sp")

    def __init__(self):
        self.q = {e: [] for e in self.ENGS}
        self.cnt = {e: 0 for e in self.ENGS}
        self.seen = {e: {} for e in self.ENGS}

    def op(self, eng, fn, reads=(), writes=(), dma=None):
        deps = {}

        def add(k, v):
            if eng == "pe" and k == "pe":
                return
            if deps.get(k, 0) < v:
                deps[k] = v

        for b in reads:
            if b.w is not None:
                add(*b.w)
            if b.excl:
                for k, v in b.r.items():
                    if k != eng:
                        add(k, v)
        for b in writes:
            if b.w is not None:
                add(*b.w)
            for k, v in b.r.items():
                add(k, v)
        waits = []
        seen = self.seen[eng]
        for k, v in deps.items():
            if seen.get(k, 0) < v:
                seen[k] = v
                waits.append((k, v))
        if dma is None:
            self.cnt[eng] += 1
            me = (eng, self.cnt[eng])
        else:
            dma.val += 16
            me = (dma, dma.val)
        self.q[eng].append((waits, fn, me))
        for b in reads:
            if b.r.get(me[0], 0) < me[1]:
                b.r[me[0]] = me[1]
        for b in writes:
            b.w = me
            b.r = {}
        return me


def build(NSEQ=2, S=2048, L=4):
    NT = S // TT
    NCH = S // 128
    NSMALL = L * PL + 8 + 8 * NSEQ
    nc = bass.Bass("TRN2", target_bir_lowering=False, dynamic_dma_scratch_size=2048)

    def dram(name, shape, kind="ExternalInput"):
        return nc.dram_tensor(name, shape, F32, kind=kind).ap()

    xT_d = dram("xT", [NSEQ, 128, 8, S])
    small_d = dram("small", [128, NSMALL])
    win_d = dram("win", [L, 96, 128, 8, 128])
    wout_d = dram("wout", [L, 8, 128, 8, 128])
    wmod_d = dram("wmod", [L, 24, 128, 8, 128])
    wg_d = dram("wg", [L, 128, 8, 128])
    wst_d = dram("wst", [L, 128, 8, 128])
    sgub_d = dram("sgub", [L, 1, 8, 128])
    out_d = dram("outT", [NSEQ, 128, 8, S], kind="ExternalOutput")

    sc = Sched()
    es = ExitStack()
    with es:
        def sb(name, shape, dt):
            return es.enter_context(nc.sbuf_tensor(name, shape, dt))

        def mksem(name):
            return DSem(es.enter_context(nc.semaphore(name)))

        esem = {e: es.enter_context(nc.semaphore("s_" + e)) for e in ("pe", "act", "dve", "pool")}

        xT = sb("xT_sb", [128, 8, S], F32)
        hT = sb("hT_sb", [128, 8, S], BF16)
        mT = sb("mT_sb", [128, 8, S], BF16)
        ring = sb("ring", [128, NW, 8, 128], BF16)
        stage = sb("stage", [128, NSTG, 8, 128], F32)
        tmps = sb("tmps", [128, NTMP, TT], F32)
        cv = sb("cv", [128, 3 + S], BF16)
        xcb = sb("xcb", [128, 2, TT], BF16)
        vn = sb("vn", [128, 1, NCH, 128], BF16)
        gbf = sb("gbf", [128, 8, 128], BF16)
        wstb = sb("wstb", [128, 8, 128], BF16)
        brow = sb("brow", [128, 8, 128], BF16)
        small = sb("small_sb", [128, NSMALL], F32)
        coef = sb("coef", [128, L * CL], F32)
        coefAG = sb("coefAG", [128, 2, 16], F32)
        modT = sb("modT", [128, L, 24, NSEQ], F32)
        cact = sb("cact", [128, 8, NSEQ], BF16)
        ident = sb("ident", [128, 128], F32)
        ones_bf = sb("ones_bf", [128, 128], BF16)
        onesrow = sb("onesrow", [128, 128], BF16)
        diagA = sb("diagA", [128, 2, 3, 128], BF16)
        diagR = sb("diagR", [128, 2, 4, 128], BF16)
        bd = sb("bd", [128, 2, 2, 128], BF16)
        stats = sb("stats", [128, NCH, 6], F32)
        mv = sb("mv", [128, NCH, 2], F32)
        lnr = sb("lnr", [128, 2, NCH], F32)
        ltmp = sb("ltmp", [128, 8], F32)
        banks = [es.enter_context(nc.psum_tensor("ps%d" % i, [128, TT], F32)) for i in range(8)]

        xb = [Buf("x%d" % t) for t in range(NT)]
        hb = [Buf("h%d" % t) for t in range(NT)]
        mb = [[Buf("m%d_%d" % (k, t)) for t in range(NT)] for k in range(KC)]
        ringb = [Buf("ring%d" % i) for i in range(NW)]
        stageb = [Buf("stage%d" % i) for i in range(NSTG)]
        tmpb = [Buf("tmp%d" % i) for i in range(NTMP)]
        cvb = [Buf("cv%d" % t) for t in range(NT)]
        cvhalo = Buf("cvhalo")
        xcbb = [Buf("xcb0"), Buf("xcb1")]
        vnb = [Buf("vn0"), Buf("vn0b")]
        vnb[1] = vnb[0]
        gbfb, wstbb, browb, smallb = Buf("gbf"), Buf("wstb"), Buf("brow"), Buf("small")
        coefb, modb, cactb, constb = Buf("coef"), Buf("mod"), Buf("cact"), Buf("const")
        coefAGb = [Buf("cag0"), Buf("cag1")]
        diagAb = [Buf("dA0"), Buf("dA1")]
        diagRb = [Buf("dR0"), Buf("dR1")]
        bdb = [Buf("bd0"), Buf("bd1")]
        statb = Buf("stats")
        bankb = [Buf("bank%d" % i, excl=True) for i in range(8)]

        stage_sem = [mksem("stg%d" % i) for i in range(NSTG)]
        x_sem = [mksem("xs%d" % t) for t in range(NT)]
        NOUT = max(1, (8 * S // 2) // (8 * TT))
        out_sem = [mksem("os%d" % i) for i in range(2 * NT)]
        outb = [Buf("outst%d" % i) for i in range(NOUT)]
        small_sem = mksem("smalls")
        ring_sem = [mksem("rg%d" % i) for i in range(NW)]
        outst = mT[:].bitcast(F32)

        st = {"bank": 0, "chunk": 0}
        free_tmps = list(range(NTMP))

        def talloc():
            assert free_tmps, "out of temporaries"
            i = free_tmps.pop(0)
            return i

        def tfree(i):
            free_tmps.append(i)

        def T(i):
            return tmps[:, i, :]

        def nbank():
            i = st["bank"]
            st["bank"] = (i + 1) % 8
            return i

        def col(c):
            return small[:, c:c + 1]

        def tsl(t):
            return slice(t * TT, (t + 1) * TT)

        def act(out, in_, func, reads, writes, bias=None, scale=None):
            kw = {}
            if bias is not None:
                kw["bias"] = bias
            if scale is not None:
                kw["scale"] = scale
            sc.op("act", lambda e: e.activation(out, in_, func, **kw), reads, writes)

        def tt(eng, out, a, b, op, reads, writes):
            sc.op(eng, lambda e: e.tensor_tensor(out, a, b, op), reads, writes)

        def stt(out, in0, scalar, in1, op0, op1, reads, writes):
            sc.op("dve", lambda e: e.scalar_tensor_tensor(out, in0, scalar, in1, op0, op1), reads, writes)

        def tsc(eng, out, in0, s1, s2, op0, op1, reads, writes):
            if s2 is None:
                sc.op(eng, lambda e: e.tensor_scalar(out, in0, s1, None, op0), reads, writes)
            else:
                sc.op(eng, lambda e: e.tensor_scalar(out, in0, s1, s2, op0, op1), reads, writes)

        def mm(out, lhsT, rhs, start, stop, reads, writes):
            sc.op("pe", lambda e: e.matmul(out, lhsT, rhs, start=start, stop=stop), reads, writes)

        def load_chunk(kind, src, extra=None):
            i = st["chunk"]
            st["chunk"] += 1
            s = i % NSTG
            if kind == "sgub":
                sc.op("sp", lambda e: e.dma_start(out=stage[0:1, s], in_=src), (), [stageb[s]], dma=stage_sem[s])
                sc.op("sp", lambda e: e.dma_start(out=stage[32:33, s], in_=src), (), [stageb[s]], dma=stage_sem[s])
            else:
                sc.op("sp", lambda e: e.dma_start(out=stage[:, s], in_=src), (), [stageb[s]], dma=stage_sem[s])
            return s

        ring_pos = {"i": 0}

        load_q = []

        def load_weight(src, gi=0):
            r = ring_pos["i"] % NW
            ring_pos["i"] += 1
            load_q.append((gi, lambda: sc.op("pool", lambda e: e.dma_start(out=ring[:, r], in_=src), (),
                                             [ringb[r]], dma=ring_sem[r])))
            return r

        def flush_loads(upto_gi=None, n=None):
            while load_q:
                if upto_gi is not None:
                    if load_q[0][0] > upto_gi:
                        break
                elif n is not None:
                    if n <= 0:
                        break
                    n -= 1
                load_q.pop(0)[1]()

        sc.op("sp", lambda e: e.dma_start(out=small[:], in_=small_d), (), [smallb], dma=small_sem)
        sc.op("pool", lambda e: e.memset(ident[:], 1.0), (), [constb])
        sc.op("pool", lambda e: e.affine_select(ident[:], ident[:], [[1, 128]], ALU.is_equal, 0.0,
                                                base=0, channel_multiplier=-1), [constb], [constb])
        sc.op("pool", lambda e: e.memset(ones_bf[:], 1.0), (), [constb])
        sc.op("pool", lambda e: e.memset(onesrow[:], 0.0), (), [constb])
        sc.op("pool", lambda e: e.memset(onesrow[0:1, :], 1.0), [constb], [constb])
        sc.op("pool", lambda e: e.memset(onesrow[32:33, :], 1.0), [constb], [constb])
        sc.op("pool", lambda e: e.memset(brow[:], 0.0), (), [browb])
        sc.op("pool", lambda e: e.memset(bd[:], 0.0), (), [bdb[0], bdb[1]])
        sc.op("pool", lambda e: e.memset(cv[:, 0:3], 0.0), (), [cvhalo])
        cbase = L * PL + 8
        act(cact[:], small[:, cbase:cbase + 8 * NSEQ].rearrange("p (k b) -> p k b", b=NSEQ), AF.Silu,
            [smallb], [cactb])
        for l in range(L):
            b0 = l * PL
            c0 = l * CL
            tsc("dve", coef[:, c0:c0 + 8], small[:, b0 + 96:b0 + 104], 0.5, None, ALU.mult, None, [smallb], [coefb])
            tsc("dve", coef[:, c0 + 8:c0 + 16], small[:, b0 + 104:b0 + 112], 0.5, None, ALU.mult, None,
                [smallb], [coefb])
            tsc("dve", coef[:, c0 + 16:c0 + 24], small[:, b0 + 88:b0 + 96], 0.5, None, ALU.mult, None,
                [smallb], [coefb])
            act(ltmp[:], small[:, b0 + 112:b0 + 120], AF.Exp, [smallb], [statb], scale=-1.0)
            act(ltmp[:], ltmp[:], AF.Ln, [statb], [statb], bias=1.0)
            tsc("dve", coef[:, c0 + 24:c0 + 32], ltmp[:], -4.0, None, ALU.mult, None, [statb], [coefb])

        def grp_weights(kind, l, j):
            if kind == "A":
                blks = [B_AX, B_AC, B_AB, B_AZ, B_GA, B_SV]
            elif kind == "R":
                blks = [B_RX, B_RZ, B_GR]
            elif kind == "S":
                blks = [B_SU, B_SZ, B_GS]
            else:
                return [wout_d[l, j]]
            return [win_d[l, b * 8 + j] for b in blks]

        def main_mm(bk, r, t):
            for k in range(KC):
                mm(banks[bk][:], ring[:, r, k, :], hT[:, k, tsl(t)], k == 0, k == KC - 1,
                   [ringb[r], hb[t]], [bankb[bk]])

        def gate_pair(bz, bg, raw):
            t1, t2 = talloc(), talloc()
            act(T(t1), banks[bz][:], AF.Tanh, [bankb[bz]], [tmpb[t1]], scale=0.5)
            act(T(t2), banks[bg][:], AF.Tanh, [bankb[bg]], [tmpb[t2]], scale=0.5)
            stt(T(t1), T(t1), 1.0, banks[bz][:], ALU.add, ALU.mult, [tmpb[t1], bankb[bz]], [tmpb[t1]])
            stt(T(t1), T(t2), 1.0, T(t1), ALU.add, ALU.mult, [tmpb[t1], tmpb[t2]], [tmpb[t1]])
            tfree(t2)
            return t1

        def unit_stages(kind, l, j, t, slots, par):
            b0 = l * PL
            c0 = l * CL
            u = {}
            if kind == "A":
                def s0():
                    bx, bc, bb_, bz, bg = (nbank() for _ in range(5))
                    for bk, r in zip((bx, bc, bb_, bz, bg), slots):
                        main_mm(bk, r, t)
                    t1 = talloc()
                    act(T(t1), banks[bx][:], AF.Copy, [bankb[bx]], [tmpb[t1]])
                    tt("dve", cv[:, 3 + t * TT:3 + (t + 1) * TT], banks[bc][:], T(t1), ALU.mult,
                       [bankb[bc], tmpb[t1]], [cvb[t]])
                    tfree(t1)
                    q = gate_pair(bz, bg, None)
                    tt("dve", T(q), banks[bb_][:], T(q), ALU.mult, [bankb[bb_], tmpb[q]], [tmpb[q]])
                    u["q"] = q
                    rv = slots[5]
                    bvv = nbank()
                    for c4 in range(4):
                        c = t * 4 + c4
                        for k in range(KC):
                            mm(banks[bvv][:, c4 * 128:(c4 + 1) * 128], hT[:, k, c * 128:(c + 1) * 128],
                               ring[:, rv, k, :], k == 0, k == KC - 1, [hb[t], ringb[rv]], [bankb[bvv]])
                    tv = talloc()
                    act(T(tv), banks[bvv][:], AF.Copy, [bankb[bvv]], [tmpb[tv]])
                    for c4 in range(4):
                        c = t * 4 + c4
                        src = T(tv)[:, c4 * 128:(c4 + 1) * 128]
                        sc.op("dve", lambda e, c=c, src=src: e.bn_stats(stats[:, c, :], src), [tmpb[tv]], [statb])
                        sc.op("dve", lambda e, c=c: e.bn_aggr(mv[:, c, :], stats[:, c, :]), [statb], [statb])
                    c0_ = t * 4
                    act(lnr[:, 0, c0_:c0_ + 4], mv[:, c0_:c0_ + 4, 1], AF.Sqrt, [statb], [lnrb], bias=EPS)
                    sc.op("dve", lambda e: e.reciprocal(lnr[:, 0, c0_:c0_ + 4], lnr[:, 0, c0_:c0_ + 4]), [lnrb], [lnrb])
                    stt(lnr[:, 1, c0_:c0_ + 4], mv[:, c0_:c0_ + 4, 0], -1.0, lnr[:, 0, c0_:c0_ + 4], ALU.mult, ALU.mult,
                        [statb, lnrb], [lnrb])
                    for c4 in range(4):
                        c = c0_ + c4
                        tsc("pool", vn[:, 0, c, :], T(tv)[:, c4 * 128:(c4 + 1) * 128],
                            lnr[:, 0, c:c + 1], lnr[:, 1, c:c + 1], ALU.mult, ALU.add,
                            [tmpb[tv], lnrb], [vnb[par]])
                    tfree(tv)

                def s1():
                    bv = nbank()
                    rd = [diagAb[par], cvb[t], cvhalo] + ([cvb[t - 1]] if t > 0 else [])
                    for tap in range(3):
                        mm(banks[bv][:], diagA[:, par, tap, :], cv[:, 1 + t * TT + tap:1 + t * TT + tap + TT],
                           tap == 0, tap == 2, rd, [bankb[bv]])
                    q = u["q"]
                    tt("dve", mT[:, j, tsl(t)], banks[bv][:], T(q), ALU.mult, [bankb[bv], tmpb[q]], [mb[j][t]])
                    tfree(q)
                return [s0, s1]
            if kind == "R":
                def s0():
                    bx, bz, bg = (nbank() for _ in range(3))
                    for bk, r in zip((bx, bz, bg), slots):
                        main_mm(bk, r, t)
                    act(cv[:, 3 + t * TT:3 + (t + 1) * TT], banks[bx][:], AF.Copy, [bankb[bx]], [cvb[t]])
                    u["q"] = gate_pair(bz, bg, None)

                def s1():
                    bv = nbank()
                    rd = [diagRb[par], cvb[t], cvhalo] + ([cvb[t - 1]] if t > 0 else [])
                    for tap in range(4):
                        mm(banks[bv][:], diagR[:, par, tap, :], cv[:, t * TT + tap:t * TT + tap + TT],
                           tap == 0, tap == 3, rd, [bankb[bv]])
                    xp = t % 2
                    act(xcb[:, xp, :], banks[bv][:], AF.Identity, [bankb[bv], smallb], [xcbb[xp]],
                        bias=col(b0 + 88 + j))
                    t3 = talloc()
                    act(T(t3), banks[bv][:], AF.Identity, [bankb[bv], coefb], [tmpb[t3]],
                        bias=coef[:, c0 + 16 + j:c0 + 17 + j], scale=0.5)
                    u["xcf"] = t3

                def s2():
                    xp = t % 2
                    bra, bri = nbank(), nbank()
                    mm(banks[bra][:], bd[:, par, 0, :], xcb[:, xp, :], True, True, [bdb[par], xcbb[xp]], [bankb[bra]])
                    mm(banks[bri][:], bd[:, par, 1, :], xcb[:, xp, :], True, True, [bdb[par], xcbb[xp]], [bankb[bri]])
                    t4, t5, t6 = talloc(), talloc(), talloc()
                    act(T(t4), banks[bra][:], AF.Tanh, [bankb[bra], coefb], [tmpb[t4]],
                        bias=coef[:, c0 + j:c0 + j + 1], scale=0.5)
                    act(T(t5), banks[bri][:], AF.Tanh, [bankb[bri], coefb], [tmpb[t5]],
                        bias=coef[:, c0 + 8 + j:c0 + 9 + j], scale=0.5)
                    hc = coef[:, c0 + 24 + j:c0 + 25 + j]
                    act(T(t4), T(t4), AF.Exp, [tmpb[t4], coefb], [tmpb[t4]], bias=hc, scale=hc)
                    tt("pool", T(t6), T(t4), T(t4), ALU.mult, [tmpb[t4]], [tmpb[t6]])
                    tsc("pool", T(t6), T(t6), 0.9999999, 0.0, ALU.min, ALU.max, [tmpb[t6]], [tmpb[t6]])
                    act(T(t6), T(t6), AF.Sqrt, [tmpb[t6]], [tmpb[t6]], bias=1.0, scale=-1.0)
                    t3 = u["xcf"]
                    stt(T(t5), T(t5), 1.0, T(t3), ALU.add, ALU.mult, [tmpb[t5], tmpb[t3]], [tmpb[t5]])
                    tfree(t3)
                    tt("pool", T(t5), T(t5), T(t6), ALU.mult, [tmpb[t5], tmpb[t6]], [tmpb[t5]])
                    tfree(t6)
                    t7 = talloc()
                    prev = scan_prev.get("t")
                    if t == 0:
                        init = 0.0
                        rd = [tmpb[t4], tmpb[t5]]
                    else:
                        init = T(prev)[:, TT - 1:TT]
                        rd = [tmpb[t4], tmpb[t5], tmpb[prev]]
                    sc.op("dve", lambda e: e.tensor_tensor_scan(T(t7), T(t4), T(t5), init, ALU.mult, ALU.add),
                          rd, [tmpb[t7]])
                    if prev is not None:
                        tfree(prev)
                    tfree(t4)
                    tfree(t5)
                    q = u["q"]
                    tt("pool", T(q), T(t7), T(q), ALU.mult, [tmpb[t7], tmpb[q]], [tmpb[q]])
                    tt("dve", mT[:, j, tsl(t)], mT[:, j, tsl(t)], T(q), ALU.add, [mb[j][t], tmpb[q]], [mb[j][t]])
                    tfree(q)
                    if t == NT - 1:
                        tfree(t7)
                        scan_prev["t"] = None
                    else:
                        scan_prev["t"] = t7
                return [s0, s1, s2]
            if kind == "S":
                def s0():
                    bu, bz, bg = (nbank() for _ in range(3))
                    for bk, r in zip((bu, bz, bg), slots):
                        main_mm(bk, r, t)
                    bzf = nbank()
                    for c4 in range(4):
                        c = t * 4 + c4
                        o = banks[bzf][:, c4 * 128:(c4 + 1) * 128]
                        mm(o, vn[:, 0, c, :], wstb[:, j, :], True, False, [vnb[par], wstbb], [bankb[bzf]])
                        mm(o, onesrow[:], brow[:, j, :], False, True, [constb, browb], [bankb[bzf]])
                    q = gate_pair(bz, bg, None)
                    tt("dve", T(q), banks[bu][:], T(q), ALU.mult, [bankb[bu], tmpb[q]], [tmpb[q]])
                    tt("dve", T(q), banks[bzf][:], T(q), ALU.mult, [bankb[bzf], tmpb[q]], [tmpb[q]])
                    tt("pool", mT[:, j, tsl(t)], mT[:, j, tsl(t)], T(q), ALU.add, [mb[j][t], tmpb[q]], [mb[j][t]])
                    tfree(q)
                return [s0]
            if kind == "V":
                def s0():
                    r = slots[0]
                    vbanks = []
                    for c in range(NCH):
                        if c % 4 == 0:
                            vbanks.append(nbank())
                        bk = vbanks[-1]
                        for k in range(KC):
                            mm(banks[bk][:, (c % 4) * 128:(c % 4 + 1) * 128], hT[:, k, c * 128:(c + 1) * 128],
                               ring[:, r, k, :], k == 0, k == KC - 1, [hb[c // 4], ringb[r]], [bankb[bk]])
                    for c in range(NCH):
                        bk = vbanks[c // 4]
                        src = banks[bk][:, (c % 4) * 128:(c % 4 + 1) * 128]
                        sc.op("dve", lambda e, c=c, src=src: e.bn_stats(stats[:, c, :], src), [bankb[bk]], [statb])
                        sc.op("dve", lambda e, c=c: e.bn_aggr(mv[:, c, :], stats[:, c, :]), [statb], [statb])
                    act(lnr[:, 0, :], mv[:, :, 1], AF.Sqrt, [statb], [statb], bias=EPS)
                    sc.op("dve", lambda e: e.reciprocal(lnr[:, 0, :], lnr[:, 0, :]), [statb], [statb])
                    stt(lnr[:, 1, :], mv[:, :, 0], -1.0, lnr[:, 0, :], ALU.mult, ALU.mult, [statb], [statb])
                    for c in range(NCH):
                        bk = vbanks[c // 4]
                        src = banks[bk][:, (c % 4) * 128:(c % 4 + 1) * 128]
                        act(vn[:, 0, c, :], src, AF.Identity, [bankb[bk], statb], [vnb[par]],
                            bias=lnr[:, 1, c:c + 1], scale=lnr[:, 0, c:c + 1])
                return [s0]
            if kind == "O":
                def s0():
                    gp = u_par["ag"]
                    for o in range(KC):
                        bk = nbank()
                        r = slots[o]
                        for k in range(KC):
                            mm(banks[bk][:], ring[:, r, k, :], mT[:, k, tsl(t)], k == 0, k == KC - 1,
                               [ringb[r], mb[k][t]], [bankb[bk]])
                        stt(xT[:, o, tsl(t)], banks[bk][:], coefAG[:, gp, 8 + o:9 + o], xT[:, o, tsl(t)],
                            ALU.mult, ALU.add, [bankb[bk], coefAGb[gp], xb[t]], [xb[t]])
                return [s0]
            raise ValueError(kind)

        scan_prev = {"t": None}
        vtmps = []
        lnrb = Buf("lnr")
        u_par = {"ag": 0}

        def group_setup(kind, l, j, par):
            b0 = l * PL
            if kind == "A":
                for tap in range(3):
                    tsc("pool", diagA[:, par, tap, :], ident[:], col(b0 + 32 + tap * 8 + j), 0.0, ALU.mult, ALU.add,
                        [constb, smallb], [diagAb[par]])
            elif kind == "R":
                for tap in range(4):
                    tsc("pool", diagR[:, par, tap, :], ident[:], col(b0 + 56 + tap * 8 + j), 0.0, ALU.mult, ALU.add,
                        [constb, smallb], [diagRb[par]])
                gv = gbf[:, j, :].rearrange("p (g e) -> p g e", g=2)
                sc.op("pool", lambda e: e.tensor_copy(bd[0:64, par, :, 0:64], gv[0:64]), [gbfb], [bdb[par]])
                sc.op("pool", lambda e: e.tensor_copy(bd[64:128, par, :, 64:128], gv[64:128]), [gbfb], [bdb[par]])

        def norm_tile(t, final, l, b, ag):
            sc.op("act", lambda e: e.activation(hT[:, :, tsl(t)], xT[:, :, tsl(t)], AF.Square), [xb[t]], [hb[t]])
            bk = nbank()
            for k in range(KC):
                mm(banks[bk][:], ones_bf[:], hT[:, k, tsl(t)], k == 0, k == KC - 1, [constb, hb[t]], [bankb[bk]])
            t1 = talloc()
            act(T(t1), banks[bk][:], AF.Sqrt, [bankb[bk]], [tmpb[t1]], bias=EPS, scale=1.0 / D)
            sc.op("dve", lambda e: e.reciprocal(banks[bk][:], T(t1)), [tmpb[t1]], [bankb[bk]])
            tfree(t1)
            if not final:
                for k in range(KC):
                    t2 = talloc()
                    tt("dve", T(t2), xT[:, k, tsl(t)], banks[bk][:], ALU.mult, [xb[t], bankb[bk]], [tmpb[t2]])
                    act(hT[:, k, tsl(t)], T(t2), AF.Identity, [tmpb[t2], coefAGb[ag], modb], [hb[t]],
                        bias=modT[:, l, k, b:b + 1], scale=coefAG[:, ag, k:k + 1])
                    tfree(t2)
            else:
                fg = L * PL
                H2 = TT // 2
                mview = mT[:, :, tsl(t)].bitcast(F32)
                hview = hT[:, :, tsl(t)].bitcast(F32)
                mbt = [mb[k_][t] for k_ in range(KC)]
                for k in range(KC):
                    stt(mview[:, k, :], xT[:, k, t * TT:t * TT + H2], col(fg + k), banks[bk][:, 0:H2],
                        ALU.mult, ALU.mult, [xb[t], smallb, bankb[bk]], mbt)
                    stt(hview[:, k, :], xT[:, k, t * TT + H2:(t + 1) * TT], col(fg + k), banks[bk][:, H2:TT],
                        ALU.mult, ALU.mult, [xb[t], smallb, bankb[bk]], [hb[t]])
                sc.op("sp", lambda e: e.dma_start(out=out_d[b, :, :, t * TT:t * TT + H2], in_=mview),
                      mbt, (), dma=out_sem[2 * t])
                sc.op("sp", lambda e: e.dma_start(out=out_d[b, :, :, t * TT + H2:(t + 1) * TT], in_=hview),
                      [hb[t]], (), dma=out_sem[2 * t + 1])

        groups = []
        for b in range(NSEQ):
            for l in range(L):
                if b == 0 and l == 0:
                    groups += [("M", 0, 0, q) for q in range(8)]
                for j in range(KC):
                    groups.append(("A", b, l, j))
                    groups.append(("RS", b, l, j))
                    if b == 0 and l + 1 < L:
                        groups.append(("M", 0, l + 1, j))
                groups.append(("O", b, l, 0))
        gslots = {}

        def nchunks(gi):
            if gi >= len(groups):
                return 0
            return {"A": 6, "RS": 6, "O": 8, "M": 3}[groups[gi][0]]

        def emit_loads(gi):
            if gi >= len(groups) or gi in gslots:
                return
            kind, b, l, j = groups[gi]
            if kind == "RS":
                srcs = grp_weights("R", l, j) + grp_weights("S", l, j)
            elif kind == "O":
                srcs = [wout_d[l, o] for o in range(KC)]
            elif kind == "M":
                srcs = [wmod_d[l, 3 * j + fi] for fi in range(3)]
            else:
                srcs = grp_weights(kind, l, j)
            gslots[gi] = [load_weight(src, gi) for src in srcs]

        pending = []

        def pump(new_unit):
            if new_unit is not None:
                pending.insert(0, [new_unit, 0])
            for ent in list(pending):
                stages, i = ent
                stages[i]()
                ent[1] += 1
            pending[:] = [e for e in pending if e[1] < len(e[0])]

        def drain():
            while pending:
                pump(None)

        def layer_setup(b, l):
            ag = (b * L + l) % 2
            stt(coefAG[:, ag, 0:8], modT[:, l, 8:16, b], 1.0, small[:, l * PL:l * PL + 8], ALU.add, ALU.mult,
                [modb, smallb], [coefAGb[ag]])
            tsc("dve", coefAG[:, ag, 8:16], modT[:, l, 16:24, b], 0.25, None, ALU.mult, None,
                [modb], [coefAGb[ag]])
            s = load_chunk("wst", wst_d[l])
            sc.op("pool", lambda e, s=s: e.affine_select(wstb[:], stage[:, s], [[0, 8], [1, 128]], ALU.is_ge, 0.0,
                                                         base=0, channel_multiplier=-1), [stageb[s]], [wstbb])
            s = load_chunk("sgub", sgub_d[l])
            sc.op("dve", lambda e, s=s: e.tensor_copy(brow[0:1], stage[0:1, s]), [stageb[s]], [browb])
            sc.op("dve", lambda e, s=s: e.tensor_copy(brow[32:33], stage[32:33, s]), [stageb[s]], [browb])
            sc.op("dve", lambda e, s=s: e.tensor_tensor(brow[32:33], stage[32:33, s], brow[32:33], ALU.subtract),
                  [stageb[s], browb], [browb])
            s = load_chunk("wg", wg_d[l])
            sc.op("pool", lambda e, s=s: e.tensor_copy(gbf[:], stage[:, s]), [stageb[s]], [gbfb])
            return ag

        def prefetch(gi):
            emit_loads(gi)
            emit_loads(gi + 1)
            if nchunks(gi) + nchunks(gi + 1) + nchunks(gi + 2) <= NW:
                emit_loads(gi + 2)
            flush_loads(upto_gi=gi)


        def run_M(lt, q, slots):
            bk = nbank()
            for fi in range(3):
                for k in range(KC):
                    mm(banks[bk][:, fi * NSEQ:(fi + 1) * NSEQ], ring[:, slots[fi], k, :], cact[:, k, :],
                       k == 0, k == KC - 1, [ringb[slots[fi]], cactb], [bankb[bk]])
            pv = banks[bk][:, 0:3 * NSEQ].rearrange("p (f b) -> p f b", b=NSEQ)
            c0 = lt * PL + 8 + 3 * q
            for b2 in range(NSEQ):
                tt("dve", modT[:, lt, 3 * q:3 * q + 3, b2], pv[:, :, b2], small[:, c0:c0 + 3], ALU.add,
                   [bankb[bk], smallb], [modb])

        def xload(b, t):
            sc.op("sp", lambda e: e.dma_start(out=xT[:, :, tsl(t)], in_=xT_d[b, :, :, tsl(t)]),
                  (), [xb[t]], dma=x_sem[t])

        gi = 0
        emit_loads(0)
        ag = None
        start_norms = []
        for b in range(NSEQ):
            if b == 0:
                for t in range(NT):
                    xload(0, t)
                while groups[gi][0] == "M":
                    prefetch(gi)
                    run_M(groups[gi][2], groups[gi][3], gslots.pop(gi))
                    gi += 1
                ag = layer_setup(0, 0)
                for t in range(min(2, NT)):
                    norm_tile(t, False, 0, 0, ag)
                start_norms = list(range(min(2, NT), NT))
            for l in range(L):
                while groups[gi][0] != "O":
                    kind, gb, gl, j = groups[gi]
                    prefetch(gi)
                    slots = gslots.pop(gi)
                    par = j % 2
                    if kind == "M":
                        run_M(gl, j, slots)
                    elif kind == "A":
                        assert (gb, gl) == (b, l)
                        group_setup("A", l, j, par)
                        group_setup("R", l, j, par)
                        for t in range(NT):
                            pump(unit_stages("A", l, j, t, slots, par))
                            flush_loads(n=3)
                            if start_norms:
                                norm_tile(start_norms.pop(0), False, 0, 0, ag)
                    else:
                        assert (gb, gl) == (b, l)
                        for t in range(NT):
                            pump(unit_stages("R", l, j, t, slots[:3], par))
                            flush_loads(n=1)
                            pump(unit_stages("S", l, j, t, slots[3:], par))
                            flush_loads(n=2)
                    gi += 1
                prefetch(gi)
                slots = gslots.pop(gi)
                u_par["ag"] = ag
                last = l + 1 == L
                if not last:
                    nag = layer_setup(b, l + 1)
                elif b + 1 < NSEQ:
                    nag = layer_setup(b + 1, 0)
                else:
                    nag = None
                def tail1(t):
                    if not last:
                        norm_tile(t, False, l + 1, b, nag)
                    else:
                        norm_tile(t, True, L, b, 0)
                        if b + 1 < NSEQ:
                            xload(b + 1, t)

                def tail2(t):
                    if last and b + 1 < NSEQ:
                        norm_tile(t, False, 0, b + 1, nag)

                for t in range(NT):
                    pump(unit_stages("O", l, 0, t, slots, 0))
                    flush_loads(n=3)
                    if t == 0:
                        drain()
                    if t >= 1:
                        tail1(t - 1)
                    if t >= 2:
                        tail2(t - 2)
                tail1(NT - 1)
                if NT >= 2:
                    tail2(NT - 2)
                tail2(NT - 1)
                gi += 1
                ag = nag
        sc.q["sp"].append(([(s_, s_.val) for s_ in out_sem if s_.val > 0], None, None))

        def semh(k):
            return k.h if isinstance(k, DSem) else esem[k]

        def replay(name, e):
            for waits, fn, me in sc.q[name]:
                for k, v in waits:
                    e.wait_ge(semh(k), v)
                if fn is None:
                    continue
                ins = fn(e)
                if isinstance(me[0], DSem):
                    ins.then_inc(me[0].h, 16)
                else:
                    ins.then_inc(esem[me[0]], 1)

        with nc.Block() as block:
            @block.tensor
            def _(e):
                replay("pe", e)

            @block.scalar
            def _(e):
                replay("act", e)

            @block.vector
            def _(e):
                replay("dve", e)

            @block.gpsimd
            def _(e):
                replay("pool", e)

            @block.sync
            def _(e):
                replay("sp", e)
    return nc


def _chunk_cols(w):
    L, K, C = w.shape
    return np.ascontiguousarray(w.reshape(L, 8, 128, C // 128, 128).transpose(0, 3, 2, 1, 4))


def _feat(v):
    sh = v.shape[:-1]
    a = v.reshape(sh + (8, 128))
    return np.moveaxis(a, -1, 0)


def prep_shared(norm_gain, w_mod, b_mod, w_in, w_out, conv_a_w, sgu_w, sgu_b, lru_conv_w, lru_conv_b,
                lru_wa, lru_ba, lru_wx, lru_bx, lru_lambda, final_gain):
    L = w_in.shape[0]
    f32 = np.float32
    sm = np.zeros((128, L, PL), f32)
    sm[:, :, 0:8] = _feat(norm_gain)
    sm[:, :, 8:32] = np.moveaxis(b_mod.reshape(L, 24, 128), -1, 0)
    sm[:, :, 32:56] = _feat(conv_a_w).reshape(128, L, 24)
    sm[:, :, 56:88] = _feat(lru_conv_w).reshape(128, L, 32)
    sm[:, :, 88:96] = _feat(lru_conv_b)
    sm[:, :, 96:104] = _feat(lru_ba.reshape(L, 1024))
    sm[:, :, 104:112] = _feat(lru_bx.reshape(L, 1024))
    sm[:, :, 112:120] = _feat(lru_lambda)
    shared = {
        "small_layers": sm.reshape(128, L * PL),
        "fgain": np.ascontiguousarray(_feat(final_gain)),
        "win": _chunk_cols(w_in),
        "wout": _chunk_cols(w_out),
        "wmod": _chunk_cols(w_mod),
        "wg": np.ascontiguousarray(
            np.stack([lru_wa, lru_wx], axis=0).reshape(2, L, 8, 2, 64, 64).transpose(1, 3, 4, 2, 0, 5)
        ).reshape(L, 128, 8, 128),
        "wst": np.ascontiguousarray(sgu_w.transpose(0, 3, 1, 2)),
        "sgub": np.ascontiguousarray(sgu_b.reshape(L, 1, 8, 128)),
    }
    return shared


def make_in_maps(x, c, shared, n_cores, NSEQ):
    in_maps = []
    B, S, _ = x.shape
    for i in range(n_cores):
        xs = x[i * NSEQ:(i + 1) * NSEQ]
        xT = np.ascontiguousarray(xs.reshape(NSEQ, S, 8, 128).transpose(0, 3, 2, 1))
        cs = c[i * NSEQ:(i + 1) * NSEQ]
        cT = cs.reshape(NSEQ, 8, 128).transpose(2, 1, 0).reshape(128, 8 * NSEQ)
        small = np.ascontiguousarray(
            np.concatenate([shared["small_layers"], shared["fgain"], cT], axis=1).astype(np.float32))
        m = {"xT": xT, "small": small}
        for k in ("win", "wout", "wmod", "wg", "wst", "sgub"):
            m[k] = shared[k]
        in_maps.append(m)
    return in_maps


_NC_CACHE = {}


def kernel(x, c, norm_gain, w_mod, b_mod, w_in, w_out, conv_a_w, sgu_w, sgu_b, lru_conv_w, lru_conv_b,
           lru_wa, lru_ba, lru_wx, lru_bx, lru_lambda, final_gain):
    args = [np.asarray(a, dtype=np.float32) for a in
            (norm_gain, w_mod, b_mod, w_in, w_out, conv_a_w, sgu_w, sgu_b, lru_conv_w, lru_conv_b,
             lru_wa, lru_ba, lru_wx, lru_bx, lru_lambda, final_gain)]
    x = np.asarray(x, dtype=np.float32)
    c = np.asarray(c, dtype=np.float32)
    B, S, _ = x.shape
    L = args[3].shape[0]
    n_cores = 8
    NSEQ = B // n_cores
    shared = prep_shared(*args)
    in_maps = make_in_maps(x, c, shared, n_cores, NSEQ)
    key = (NSEQ, S, L)
    if key not in _NC_CACHE:
        _NC_CACHE[key] = build(NSEQ, S, L)
    nc = _NC_CACHE[key]
    res = run_bass_kernel_spmd(nc, in_maps, core_ids=list(range(n_cores)))
    out = np.empty((B, S, D), np.float32)
    for i in range(n_cores):
        oT = np.asarray(res.results[i]["outT"]).reshape(NSEQ, 128, 8, S)
        out[i * NSEQ:(i + 1) * NSEQ] = oT.transpose(0, 3, 2, 1).reshape(NSEQ, S, D)
    return out
```

```python
import numpy as np
from contextlib import ExitStack
import concourse.bass as bass
import concourse.mybir as mybir
from concourse.bass_utils import run_bass_kernel_spmd

F32 = mybir.dt.float32
BF16 = mybir.dt.bfloat16
AF = mybir.ActivationFunctionType
ALU = mybir.AluOpType

D = 1024
KC = 8
TT = 512
EPS = 1e-6
PL = 120
CL = 32
NW = 18
NSTG = 2
NTMP = 12
B_AX, B_AB, B_AC, B_AZ, B_SU, B_SV, B_SZ, B_RX, B_RZ, B_GA, B_GS, B_GR = range(12)


class DSem:
    def __init__(self, h):
        self.h = h
        self.val = 0


class Buf:
    __slots__ = ("name", "w", "r", "excl")

    def __init__(self, name, excl=False):
        self.name = name
        self.w = None
        self.r = {}
        self.excl = excl


class Sched:
    ENGS = ("pe", "act", "dve", "pool", "sp")

    def __init__(self):
        self.q = {e: [] for e in self.ENGS}
        self.cnt = {e: 0 for e in self.ENGS}
        self.seen = {e: {} for e in self.ENGS}

    def op(self, eng, fn, reads=(), writes=(), dma=None):
        deps = {}

        def add(k, v):
            if eng == "pe" and k == "pe":
                return
            if deps.get(k, 0) < v:
                deps[k] = v

        for b in reads:
            if b.w is not None:
                add(*b.w)
            if b.excl:
                for k, v in b.r.items():
                    if k != eng:
                        add(k, v)
        for b in writes:
            if b.w is not None:
                add(*b.w)
            for k, v in b.r.items():
                add(k, v)
        waits = []
        seen = self.seen[eng]
        for k, v in deps.items():
            if seen.get(k, 0) < v:
                seen[k] = v
                waits.append((k, v))
        if dma is None:
            self.cnt[eng] += 1
            me = (eng, self.cnt[eng])
        else:
            dma.val += 16
            me = (dma, dma.val)
        self.q[eng].append((waits, fn, me))
        for b in reads:
            if b.r.get(me[0], 0) < me[1]:
                b.r[me[0]] = me[1]
        for b in writes:
            b.w = me
            b.r = {}
        return me


def build(NSEQ=2, S=2048, L=4):
    NT = S // TT
    NCH = S // 128
    NSMALL = L * PL + 8 + 8 * NSEQ
    nc = bass.Bass("TRN2", target_bir_lowering=False, dynamic_dma_scratch_size=2048)

    def dram(name, shape, kind="ExternalInput"):
        return nc.dram_tensor(name, shape, F32, kind=kind).ap()

    xT_d = dram("xT", [NSEQ, 128, 8, S])
    small_d = dram("small", [128, NSMALL])
    win_d = dram("win", [L, 96, 128, 8, 128])
    wout_d = dram("wout", [L, 8, 128, 8, 128])
    wmod_d = dram("wmod", [L, 24, 128, 8, 128])
    wg_d = dram("wg", [L, 128, 8, 128])
    wst_d = dram("wst", [L, 128, 8, 128])
    sgub_d = dram("sgub", [L, 1, 8, 128])
    out_d = dram("outT", [NSEQ, 128, 8, S], kind="ExternalOutput")

    sc = Sched()
    es = ExitStack()
    with es:
        def sb(name, shape, dt):
            return es.enter_context(nc.sbuf_tensor(name, shape, dt))

        def mksem(name):
            return DSem(es.enter_context(nc.semaphore(name)))

        esem = {e: es.enter_context(nc.semaphore("s_" + e)) for e in ("pe", "act", "dve", "pool")}

        xT = sb("xT_sb", [128, 8, S], F32)
        hT = sb("hT_sb", [128, 8, S], BF16)
        mT = sb("mT_sb", [128, 8, S], BF16)
        ring = sb("ring", [128, NW, 8, 128], BF16)
        stage = sb("stage", [128, NSTG, 8, 128], F32)
        tmps = sb("tmps", [128, NTMP, TT], F32)
        cv = sb("cv", [128, 3 + S], BF16)
        xcb = sb("xcb", [128, 2, TT], BF16)
        vn = sb("vn", [128, 1, NCH, 128], BF16)
        gbf = sb("gbf", [128, 8, 128], BF16)
        wstb = sb("wstb", [128, 8, 128], BF16)
        brow = sb("brow", [128, 8, 128], BF16)
        small = sb("small_sb", [128, NSMALL], F32)
        coef = sb("coef", [128, L * CL], F32)
        coefAG = sb("coefAG", [128, 2, 16], F32)
        modT = sb("modT", [128, L, 24, NSEQ], F32)
        cact = sb("cact", [128, 8, NSEQ], BF16)
        ident = sb("ident", [128, 128], F32)
        ones_bf = sb("ones_bf", [128, 128], BF16)
        onesrow = sb("onesrow", [128, 128], BF16)
        diagA = sb("diagA", [128, 2, 3, 128], BF16)
        diagR = sb("diagR", [128, 2, 4, 128], BF16)
        bd = sb("bd", [128, 2, 2, 128], BF16)
        stats = sb("stats", [128, NCH, 6], F32)
        mv = sb("mv", [128, NCH, 2], F32)
        lnr = sb("lnr", [128, 2, NCH], F32)
        ltmp = sb("ltmp", [128, 8], F32)
        banks = [es.enter_context(nc.psum_tensor("ps%d" % i, [128, TT], F32)) for i in range(8)]

        xb = [Buf("x%d" % t) for t in range(NT)]
        hb = [Buf("h%d" % t) for t in range(NT)]
        mb = [[Buf("m%d_%d" % (k, t)) for t in range(NT)] for k in range(KC)]
        ringb = [Buf("ring%d" % i) for i in range(NW)]
        stageb = [Buf("stage%d" % i) for i in range(NSTG)]
        tmpb = [Buf("tmp%d" % i) for i in range(NTMP)]
        cvb = [Buf("cv%d" % t) for t in range(NT)]
        cvhalo = Buf("cvhalo")
        xcbb = [Buf("xcb0"), Buf("xcb1")]
        vnb = [Buf("vn0"), Buf("vn0b")]
        vnb[1] = vnb[0]
        gbfb, wstbb, browb, smallb = Buf("gbf"), Buf("wstb"), Buf("brow"), Buf("small")
        coefb, modb, cactb, constb = Buf("coef"), Buf("mod"), Buf("cact"), Buf("const")
        coefAGb = [Buf("cag0"), Buf("cag1")]
        diagAb = [Buf("dA0"), Buf("dA1")]
        diagRb = [Buf("dR0"), Buf("dR1")]
        bdb = [Buf("bd0"), Buf("bd1")]
        statb = Buf("stats")
        bankb = [Buf("bank%d" % i, excl=True) for i in range(8)]

        stage_sem = [mksem("stg%d" % i) for i in range(NSTG)]
        x_sem = [mksem("xs%d" % t) for t in range(NT)]
        NOUT = max(1, (8 * S // 2) // (8 * TT))
        out_sem = [mksem("os%d" % i) for i in range(2 * NT)]
        outb = [Buf("outst%d" % i) for i in range(NOUT)]
        small_sem = mksem("smalls")
        ring_sem = [mksem("rg%d" % i) for i in range(NW)]
        outst = mT[:].bitcast(F32)

        st = {"bank": 0, "chunk": 0}
        free_tmps = list(range(NTMP))

        def talloc():
            assert free_tmps, "out of temporaries"
            i = free_tmps.pop(0)
            return i

        def tfree(i):
            free_tmps.append(i)

        def T(i):
            return tmps[:, i, :]

        def nbank():
            i = st["bank"]
            st["bank"] = (i + 1) % 8
            return i

        def col(c):
            return small[:, c:c + 1]

        def tsl(t):
            return slice(t * TT, (t + 1) * TT)

        def act(out, in_, func, reads, writes, bias=None, scale=None):
            kw = {}
            if bias is not None:
                kw["bias"] = bias
            if scale is not None:
                kw["scale"] = scale
            sc.op("act", lambda e: e.activation(out, in_, func, **kw), reads, writes)

        def tt(eng, out, a, b, op, reads, writes):
            sc.op(eng, lambda e: e.tensor_tensor(out, a, b, op), reads, writes)

        def stt(out, in0, scalar, in1, op0, op1, reads, writes):
            sc.op("dve", lambda e: e.scalar_tensor_tensor(out, in0, scalar, in1, op0, op1), reads, writes)

        def tsc(eng, out, in0, s1, s2, op0, op1, reads, writes):
            if s2 is None:
                sc.op(eng, lambda e: e.tensor_scalar(out, in0, s1, None, op0), reads, writes)
            else:
                sc.op(eng, lambda e: e.tensor_scalar(out, in0, s1, s2, op0, op1), reads, writes)

        def mm(out, lhsT, rhs, start, stop, reads, writes):
            sc.op("pe", lambda e: e.matmul(out, lhsT, rhs, start=start, stop=stop), reads, writes)

        def load_chunk(kind, src, extra=None):
            i = st["chunk"]
            st["chunk"] += 1
            s = i % NSTG
            if kind == "sgub":
                sc.op("sp", lambda e: e.dma_start(out=stage[0:1, s], in_=src), (), [stageb[s]], dma=stage_sem[s])
                sc.op("sp", lambda e: e.dma_start(out=stage[32:33, s], in_=src), (), [stageb[s]], dma=stage_sem[s])
            else:
                sc.op("sp", lambda e: e.dma_start(out=stage[:, s], in_=src), (), [stageb[s]], dma=stage_sem[s])
            return s

        ring_pos = {"i": 0}

        load_q = []

        def load_weight(src, gi=0):
            r = ring_pos["i"] % NW
            ring_pos["i"] += 1
            load_q.append((gi, lambda: sc.op("pool", lambda e: e.dma_start(out=ring[:, r], in_=src), (),
                                             [ringb[r]], dma=ring_sem[r])))
            return r

        def flush_loads(upto_gi=None, n=None):
            while load_q:
                if upto_gi is not None:
                    if load_q[0][0] > upto_gi:
                        break
                elif n is not None:
                    if n <= 0:
                        break
                    n -= 1
                load_q.pop(0)[1]()

        sc.op("sp", lambda e: e.dma_start(out=small[:], in_=small_d), (), [smallb], dma=small_sem)
        sc.op("pool", lambda e: e.memset(ident[:], 1.0), (), [constb])
        sc.op("pool", lambda e: e.affine_select(ident[:], ident[:], [[1, 128]], ALU.is_equal, 0.0,
                                                base=0, channel_multiplier=-1), [constb], [constb])
        sc.op("pool", lambda e: e.memset(ones_bf[:], 1.0), (), [constb])
        sc.op("pool", lambda e: e.memset(onesrow[:], 0.0), (), [constb])
        sc.op("pool", lambda e: e.memset(onesrow[0:1, :], 1.0), [constb], [constb])
        sc.op("pool", lambda e: e.memset(onesrow[32:33, :], 1.0), [constb], [constb])
        sc.op("pool", lambda e: e.memset(brow[:], 0.0), (), [browb])
        sc.op("pool", lambda e: e.memset(bd[:], 0.0), (), [bdb[0], bdb[1]])
        sc.op("pool", lambda e: e.memset(cv[:, 0:3], 0.0), (), [cvhalo])
        cbase = L * PL + 8
        act(cact[:], small[:, cbase:cbase + 8 * NSEQ].rearrange("p (k b) -> p k b", b=NSEQ), AF.Silu,
            [smallb], [cactb])
        for l in range(L):
            b0 = l * PL
            c0 = l * CL
            tsc("dve", coef[:, c0:c0 + 8], small[:, b0 + 96:b0 + 104], 0.5, None, ALU.mult, None, [smallb], [coefb])
            tsc("dve", coef[:, c0 + 8:c0 + 16], small[:, b0 + 104:b0 + 112], 0.5, None, ALU.mult, None,
                [smallb], [coefb])
            tsc("dve", coef[:, c0 + 16:c0 + 24], small[:, b0 + 88:b0 + 96], 0.5, None, ALU.mult, None,
                [smallb], [coefb])
            act(ltmp[:], small[:, b0 + 112:b0 + 120], AF.Exp, [smallb], [statb], scale=-1.0)
            act(ltmp[:], ltmp[:], AF.Ln, [statb], [statb], bias=1.0)
            tsc("dve", coef[:, c0 + 24:c0 + 32], ltmp[:], -4.0, None, ALU.mult, None, [statb], [coefb])

        def grp_weights(kind, l, j):
            if kind == "A":
                blks = [B_AX, B_AC, B_AB, B_AZ, B_GA, B_SV]
            elif kind == "R":
                blks = [B_RX, B_RZ, B_GR]
            elif kind == "S":
                blks = [B_SU, B_SZ, B_GS]
            else:
                return [wout_d[l, j]]
            return [win_d[l, b * 8 + j] for b in blks]

        def main_mm(bk, r, t):
            for k in range(KC):
                mm(banks[bk][:], ring[:, r, k, :], hT[:, k, tsl(t)], k == 0, k == KC - 1,
                   [ringb[r], hb[t]], [bankb[bk]])

        def gate_pair(bz, bg, raw):
            t1, t2 = talloc(), talloc()
            act(T(t1), banks[bz][:], AF.Tanh, [bankb[bz]], [tmpb[t1]], scale=0.5)
            act(T(t2), banks[bg][:], AF.Tanh, [bankb[bg]], [tmpb[t2]], scale=0.5)
            stt(T(t1), T(t1), 1.0, banks[bz][:], ALU.add, ALU.mult, [tmpb[t1], bankb[bz]], [tmpb[t1]])
            stt(T(t1), T(t2), 1.0, T(t1), ALU.add, ALU.mult, [tmpb[t1], tmpb[t2]], [tmpb[t1]])
            tfree(t2)
            return t1

        def unit_stages(kind, l, j, t, slots, par):
            b0 = l * PL
            c0 = l * CL
            u = {}
            if kind == "A":
                def s0():
                    bx, bc, bb_, bz, bg = (nbank() for _ in range(5))
                    for bk, r in zip((bx, bc, bb_, bz, bg), slots):
                        main_mm(bk, r, t)
                    t1 = talloc()
                    act(T(t1), banks[bx][:], AF.Copy, [bankb[bx]], [tmpb[t1]])
                    tt("dve", cv[:, 3 + t * TT:3 + (t + 1) * TT], banks[bc][:], T(t1), ALU.mult,
                       [bankb[bc], tmpb[t1]], [cvb[t]])
                    tfree(t1)
                    q = gate_pair(bz, bg, None)
                    tt("dve", T(q), banks[bb_][:], T(q), ALU.mult, [bankb[bb_], tmpb[q]], [tmpb[q]])
                    u["q"] = q
                    rv = slots[5]
                    bvv = nbank()
                    for c4 in range(4):
                        c = t * 4 + c4
                        for k in range(KC):
                            mm(banks[bvv][:, c4 * 128:(c4 + 1) * 128], hT[:, k, c * 128:(c + 1) * 128],
                               ring[:, rv, k, :], k == 0, k == KC - 1, [hb[t], ringb[rv]], [bankb[bvv]])
                    tv = talloc()
                    act(T(tv), banks[bvv][:], AF.Copy, [bankb[bvv]], [tmpb[tv]])
                    for c4 in range(4):
                        c = t * 4 + c4
                        src = T(tv)[:, c4 * 128:(c4 + 1) * 128]
                        sc.op("dve", lambda e, c=c, src=src: e.bn_stats(stats[:, c, :], src), [tmpb[tv]], [statb])
                        sc.op("dve", lambda e, c=c: e.bn_aggr(mv[:, c, :], stats[:, c, :]), [statb], [statb])
                    c0_ = t * 4
                    act(lnr[:, 0, c0_:c0_ + 4], mv[:, c0_:c0_ + 4, 1], AF.Sqrt, [statb], [lnrb], bias=EPS)
                    sc.op("dve", lambda e: e.reciprocal(lnr[:, 0, c0_:c0_ + 4], lnr[:, 0, c0_:c0_ + 4]), [lnrb], [lnrb])
                    stt(lnr[:, 1, c0_:c0_ + 4], mv[:, c0_:c0_ + 4, 0], -1.0, lnr[:, 0, c0_:c0_ + 4], ALU.mult, ALU.mult,
                        [statb, lnrb], [lnrb])
                    for c4 in range(4):
                        c = c0_ + c4
                        tsc("pool", vn[:, 0, c, :], T(tv)[:, c4 * 128:(c4 + 1) * 128],
                            lnr[:, 0, c:c + 1], lnr[:, 1, c:c + 1], ALU.mult, ALU.add,
                            [tmpb[tv], lnrb], [vnb[par]])
                    tfree(tv)

                def s1():
                    bv = nbank()
                    rd = [diagAb[par], cvb[t], cvhalo] + ([cvb[t - 1]] if t > 0 else [])
                    for tap in range(3):
                        mm(banks[bv][:], diagA[:, par, tap, :], cv[:, 1 + t * TT + tap:1 + t * TT + tap + TT],
                           tap == 0, tap == 2, rd, [bankb[bv]])
                    q = u["q"]
                    tt("dve", mT[:, j, tsl(t)], banks[bv][:], T(q), ALU.mult, [bankb[bv], tmpb[q]], [mb[j][t]])
                    tfree(q)
                return [s0, s1]
            if kind == "R":
                def s0():
                    bx, bz, bg = (nbank() for _ in range(3))
                    for bk, r in zip((bx, bz, bg), slots):
                        main_mm(bk, r, t)
                    act(cv[:, 3 + t * TT:3 + (t + 1) * TT], banks[bx][:], AF.Copy, [bankb[bx]], [cvb[t]])
                    u["q"] = gate_pair(bz, bg, None)

                def s1():
                    bv = nbank()
                    rd = [diagRb[par], cvb[t], cvhalo] + ([cvb[t - 1]] if t > 0 else [])
                    for tap in range(4):
                        mm(banks[bv][:], diagR[:, par, tap, :], cv[:, t * TT + tap:t * TT + tap + TT],
                           tap == 0, tap == 3, rd, [bankb[bv]])
                    xp = t % 2
                    act(xcb[:, xp, :], banks[bv][:], AF.Identity, [bankb[bv], smallb], [xcbb[xp]],
                        bias=col(b0 + 88 + j))
                    t3 = talloc()
                    act(T(t3), banks[bv][:], AF.Identity, [bankb[bv], coefb], [tmpb[t3]],
                        bias=coef[:, c0 + 16 + j:c0 + 17 + j], scale=0.5)
                    u["xcf"] = t3

                def s2():
                    xp = t % 2
                    bra, bri = nbank(), nbank()
                    mm(banks[bra][:], bd[:, par, 0, :], xcb[:, xp, :], True, True, [bdb[par], xcbb[xp]], [bankb[bra]])
                    mm(banks[bri][:], bd[:, par, 1, :], xcb[:, xp, :], True, True, [bdb[par], xcbb[xp]], [bankb[bri]])
                    t4, t5, t6 = talloc(), talloc(), talloc()
                    act(T(t4), banks[bra][:], AF.Tanh, [bankb[bra], coefb], [tmpb[t4]],
                        bias=coef[:, c0 + j:c0 + j + 1], scale=0.5)
                    act(T(t5), banks[bri][:], AF.Tanh, [bankb[bri], coefb], [tmpb[t5]],
                        bias=coef[:, c0 + 8 + j:c0 + 9 + j], scale=0.5)
                    hc = coef[:, c0 + 24 + j:c0 + 25 + j]
                    act(T(t4), T(t4), AF.Exp, [tmpb[t4], coefb], [tmpb[t4]], bias=hc, scale=hc)
                    tt("pool", T(t6), T(t4), T(t4), ALU.mult, [tmpb[t4]], [tmpb[t6]])
                    tsc("pool", T(t6), T(t6), 0.9999999, 0.0, ALU.min, ALU.max, [tmpb[t6]], [tmpb[t6]])
                    act(T(t6), T(t6), AF.Sqrt, [tmpb[t6]], [tmpb[t6]], bias=1.0, scale=-1.0)
                    t3 = u["xcf"]
                    stt(T(t5), T(t5), 1.0, T(t3), ALU.add, ALU.mult, [tmpb[t5], tmpb[t3]], [tmpb[t5]])
                    tfree(t3)
                    tt("pool", T(t5), T(t5), T(t6), ALU.mult, [tmpb[t5], tmpb[t6]], [tmpb[t5]])
                    tfree(t6)
                    t7 = talloc()
                    prev = scan_prev.get("t")
                    if t == 0:
                        init = 0.0
                        rd = [tmpb[t4], tmpb[t5]]
                    else:
                        init = T(prev)[:, TT - 1:TT]
                        rd = [tmpb[t4], tmpb[t5], tmpb[prev]]
                    sc.op("dve", lambda e: e.tensor_tensor_scan(T(t7), T(t4), T(t5), init, ALU.mult, ALU.add),
                          rd, [tmpb[t7]])
                    if prev is not None:
                        tfree(prev)
                    tfree(t4)
                    tfree(t5)
                    q = u["q"]
                    tt("pool", T(q), T(t7), T(q), ALU.mult, [tmpb[t7], tmpb[q]], [tmpb[q]])
                    tt("dve", mT[:, j, tsl(t)], mT[:, j, tsl(t)], T(q), ALU.add, [mb[j][t], tmpb[q]], [mb[j][t]])
                    tfree(q)
                    if t == NT - 1:
                        tfree(t7)
                        scan_prev["t"] = None
                    else:
                        scan_prev["t"] = t7
                return [s0, s1, s2]
            if kind == "S":
                def s0():
                    bu, bz, bg = (nbank() for _ in range(3))
                    for bk, r in zip((bu, bz, bg), slots):
                        main_mm(bk, r, t)
                    bzf = nbank()
                    for c4 in range(4):
                        c = t * 4 + c4
                        o = banks[bzf][:, c4 * 128:(c4 + 1) * 128]
                        mm(o, vn[:, 0, c, :], wstb[:, j, :], True, False, [vnb[par], wstbb], [bankb[bzf]])
                        mm(o, onesrow[:], brow[:, j, :], False, True, [constb, browb], [bankb[bzf]])
                    q = gate_pair(bz, bg, None)
                    tt("dve", T(q), banks[bu][:], T(q), ALU.mult, [bankb[bu], tmpb[q]], [tmpb[q]])
                    tt("dve", T(q), banks[bzf][:], T(q), ALU.mult, [bankb[bzf], tmpb[q]], [tmpb[q]])
                    tt("pool", mT[:, j, tsl(t)], mT[:, j, tsl(t)], T(q), ALU.add, [mb[j][t], tmpb[q]], [mb[j][t]])
                    tfree(q)
                return [s0]
            if kind == "V":
                def s0():
                    r = slots[0]
                    vbanks = []
                    for c in range(NCH):
                        if c % 4 == 0:
                            vbanks.append(nbank())
                        bk = vbanks[-1]
                        for k in range(KC):
                            mm(banks[bk][:, (c % 4) * 128:(c % 4 + 1) * 128], hT[:, k, c * 128:(c + 1) * 128],
                               ring[:, r, k, :], k == 0, k == KC - 1, [hb[c // 4], ringb[r]], [bankb[bk]])
                    for c in range(NCH):
                        bk = vbanks[c // 4]
                        src = banks[bk][:, (c % 4) * 128:(c % 4 + 1) * 128]
                        sc.op("dve", lambda e, c=c, src=src: e.bn_stats(stats[:, c, :], src), [bankb[bk]], [statb])
                        sc.op("dve", lambda e, c=c: e.bn_aggr(mv[:, c, :], stats[:, c, :]), [statb], [statb])
                    act(lnr[:, 0, :], mv[:, :, 1], AF.Sqrt, [statb], [statb], bias=EPS)
                    sc.op("dve", lambda e: e.reciprocal(lnr[:, 0, :], lnr[:, 0, :]), [statb], [statb])
                    stt(lnr[:, 1, :], mv[:, :, 0], -1.0, lnr[:, 0, :], ALU.mult, ALU.mult, [statb], [statb])
                    for c in range(NCH):
                        bk = vbanks[c // 4]
                        src = banks[bk][:, (c % 4) * 128:(c % 4 + 1) * 128]
                        act(vn[:, 0, c, :], src, AF.Identity, [bankb[bk], statb], [vnb[par]],
                            bias=lnr[:, 1, c:c + 1], scale=lnr[:, 0, c:c + 1])
                return [s0]
            if kind == "O":
                def s0():
                    gp = u_par["ag"]
                    for o in range(KC):
                        bk = nbank()
                        r = slots[o]
                        for k in range(KC):
                            mm(banks[bk][:], ring[:, r, k, :], mT[:, k, tsl(t)], k == 0, k == KC - 1,
                               [ringb[r], mb[k][t]], [bankb[bk]])
                        stt(xT[:, o, tsl(t)], banks[bk][:], coefAG[:, gp, 8 + o:9 + o], xT[:, o, tsl(t)],
                            ALU.mult, ALU.add, [bankb[bk], coefAGb[gp], xb[t]], [xb[t]])
                return [s0]
            raise ValueError(kind)

        scan_prev = {"t": None}
        vtmps = []
        lnrb = Buf("lnr")
        u_par = {"ag": 0}

        def group_setup(kind, l, j, par):
            b0 = l * PL
            if kind == "A":
                for tap in range(3):
                    tsc("pool", diagA[:, par, tap, :], ident[:], col(b0 + 32 + tap * 8 + j), 0.0, ALU.mult, ALU.add,
                        [constb, smallb], [diagAb[par]])
            elif kind == "R":
                for tap in range(4):
                    tsc("pool", diagR[:, par, tap, :], ident[:], col(b0 + 56 + tap * 8 + j), 0.0, ALU.mult, ALU.add,
                        [constb, smallb], [diagRb[par]])
                gv = gbf[:, j, :].rearrange("p (g e) -> p g e", g=2)
                sc.op("pool", lambda e: e.tensor_copy(bd[0:64, par, :, 0:64], gv[0:64]), [gbfb], [bdb[par]])
                sc.op("pool", lambda e: e.tensor_copy(bd[64:128, par, :, 64:128], gv[64:128]), [gbfb], [bdb[par]])

        def norm_tile(t, final, l, b, ag):
            sc.op("act", lambda e: e.activation(hT[:, :, tsl(t)], xT[:, :, tsl(t)], AF.Square), [xb[t]], [hb[t]])
            bk = nbank()
            for k in range(KC):
                mm(banks[bk][:], ones_bf[:], hT[:, k, tsl(t)], k == 0, k == KC - 1, [constb, hb[t]], [bankb[bk]])
            t1 = talloc()
            act(T(t1), banks[bk][:], AF.Sqrt, [bankb[bk]], [tmpb[t1]], bias=EPS, scale=1.0 / D)
            sc.op("dve", lambda e: e.reciprocal(banks[bk][:], T(t1)), [tmpb[t1]], [bankb[bk]])
            tfree(t1)
            if not final:
                for k in range(KC):
                    t2 = talloc()
                    tt("dve", T(t2), xT[:, k, tsl(t)], banks[bk][:], ALU.mult, [xb[t], bankb[bk]], [tmpb[t2]])
                    act(hT[:, k, tsl(t)], T(t2), AF.Identity, [tmpb[t2], coefAGb[ag], modb], [hb[t]],
                        bias=modT[:, l, k, b:b + 1], scale=coefAG[:, ag, k:k + 1])
                    tfree(t2)
            else:
                fg = L * PL
                H2 = TT // 2
                mview = mT[:, :, tsl(t)].bitcast(F32)
                hview = hT[:, :, tsl(t)].bitcast(F32)
                mbt = [mb[k_][t] for k_ in range(KC)]
                for k in range(KC):
                    stt(mview[:, k, :], xT[:, k, t * TT:t * TT + H2], col(fg + k), banks[bk][:, 0:H2],
                        ALU.mult, ALU.mult, [xb[t], smallb, bankb[bk]], mbt)
                    stt(hview[:, k, :], xT[:, k, t * TT + H2:(t + 1) * TT], col(fg + k), banks[bk][:, H2:TT],
                        ALU.mult, ALU.mult, [xb[t], smallb, bankb[bk]], [hb[t]])
                sc.op("sp", lambda e: e.dma_start(out=out_d[b, :, :, t * TT:t * TT + H2], in_=mview),
                      mbt, (), dma=out_sem[2 * t])
                sc.op("sp", lambda e: e.dma_start(out=out_d[b, :, :, t * TT + H2:(t + 1) * TT], in_=hview),
                      [hb[t]], (), dma=out_sem[2 * t + 1])

        groups = []
        for b in range(NSEQ):
            for l in range(L):
                if b == 0 and l == 0:
                    groups += [("M", 0, 0, q) for q in range(8)]
                for j in range(KC):
                    groups.append(("A", b, l, j))
                    groups.append(("RS", b, l, j))
                    if b == 0 and l + 1 < L:
                        groups.append(("M", 0, l + 1, j))
                groups.append(("O", b, l, 0))
        gslots = {}

        def nchunks(gi):
            if gi >= len(groups):
                return 0
            return {"A": 6, "RS": 6, "O": 8, "M": 3}[groups[gi][0]]

        def emit_loads(gi):
            if gi >= len(groups) or gi in gslots:
                return
            kind, b, l, j = groups[gi]
            if kind == "RS":
                srcs = grp_weights("R", l, j) + grp_weights("S", l, j)
            elif kind == "O":
                srcs = [wout_d[l, o] for o in range(KC)]
            elif kind == "M":
                srcs = [wmod_d[l, 3 * j + fi] for fi in range(3)]
            else:
                srcs = grp_weights(kind, l, j)
            gslots[gi] = [load_weight(src, gi) for src in srcs]

        pending = []

        def pump(new_unit):
            if new_unit is not None:
                pending.insert(0, [new_unit, 0])
            for ent in list(pending):
                stages, i = ent
                stages[i]()
                ent[1] += 1
            pending[:] = [e for e in pending if e[1] < len(e[0])]

        def drain():
            while pending:
                pump(None)

        def layer_setup(b, l):
            ag = (b * L + l) % 2
            stt(coefAG[:, ag, 0:8], modT[:, l, 8:16, b], 1.0, small[:, l * PL:l * PL + 8], ALU.add, ALU.mult,
                [modb, smallb], [coefAGb[ag]])
            tsc("dve", coefAG[:, ag, 8:16], modT[:, l, 16:24, b], 0.25, None, ALU.mult, None,
                [modb], [coefAGb[ag]])
            s = load_chunk("wst", wst_d[l])
            sc.op("pool", lambda e, s=s: e.affine_select(wstb[:], stage[:, s], [[0, 8], [1, 128]], ALU.is_ge, 0.0,
                                                         base=0, channel_multiplier=-1), [stageb[s]], [wstbb])
            s = load_chunk("sgub", sgub_d[l])
            sc.op("dve", lambda e, s=s: e.tensor_copy(brow[0:1], stage[0:1, s]), [stageb[s]], [browb])
            sc.op("dve", lambda e, s=s: e.tensor_copy(brow[32:33], stage[32:33, s]), [stageb[s]], [browb])
            sc.op("dve", lambda e, s=s: e.tensor_tensor(brow[32:33], stage[32:33, s], brow[32:33], ALU.subtract),
                  [stageb[s], browb], [browb])
            s = load_chunk("wg", wg_d[l])
            sc.op("pool", lambda e, s=s: e.tensor_copy(gbf[:], stage[:, s]), [stageb[s]], [gbfb])
            return ag

        def prefetch(gi):
            emit_loads(gi)
            emit_loads(gi + 1)
            if nchunks(gi) + nchunks(gi + 1) + nchunks(gi + 2) <= NW:
                emit_loads(gi + 2)
            flush_loads(upto_gi=gi)


        def run_M(lt, q, slots):
            bk = nbank()
            for fi in range(3):
                for k in range(KC):
                    mm(banks[bk][:, fi * NSEQ:(fi + 1) * NSEQ], ring[:, slots[fi], k, :], cact[:, k, :],
                       k == 0, k == KC - 1, [ringb[slots[fi]], cactb], [bankb[bk]])
            pv = banks[bk][:, 0:3 * NSEQ].rearrange("p (f b) -> p f b", b=NSEQ)
            c0 = lt * PL + 8 + 3 * q
            for b2 in range(NSEQ):
                tt("dve", modT[:, lt, 3 * q:3 * q + 3, b2], pv[:, :, b2], small[:, c0:c0 + 3], ALU.add,
                   [bankb[bk], smallb], [modb])

        def xload(b, t):
            sc.op("sp", lambda e: e.dma_start(out=xT[:, :, tsl(t)], in_=xT_d[b, :, :, tsl(t)]),
                  (), [xb[t]], dma=x_sem[t])

        gi = 0
        emit_loads(0)
        ag = None
        start_norms = []
        deferred = []
        for b in range(NSEQ):
            if b == 0:
                for t in range(NT):
                    xload(0, t)
                while groups[gi][0] == "M":
                    prefetch(gi)
                    run_M(groups[gi][2], groups[gi][3], gslots.pop(gi))
                    gi += 1
                ag = layer_setup(0, 0)
                for t in range(min(2, NT)):
                    norm_tile(t, False, 0, 0, ag)
                start_norms = list(range(min(2, NT), NT))
            for l in range(L):
                while groups[gi][0] != "O":
                    kind, gb, gl, j = groups[gi]
                    prefetch(gi)
                    slots = gslots.pop(gi)
                    par = j % 2
                    if kind == "M":
                        run_M(gl, j, slots)
                    elif kind == "A":
                        assert (gb, gl) == (b, l)
                        group_setup("A", l, j, par)
                        group_setup("R", l, j, par)
                        for t in range(NT):
                            pump(unit_stages("A", l, j, t, slots, par))
                            flush_loads(n=3)
                            if start_norms:
                                norm_tile(start_norms.pop(0), False, 0, 0, ag)
                            if deferred:
                                deferred.pop(0)()
                    else:
                        assert (gb, gl) == (b, l)
                        for t in range(NT):
                            pump(unit_stages("R", l, j, t, slots[:3], par))
                            flush_loads(n=1)
                            pump(unit_stages("S", l, j, t, slots[3:], par))
                            flush_loads(n=2)
                    gi += 1
                prefetch(gi)
                slots = gslots.pop(gi)
                u_par["ag"] = ag
                last = l + 1 == L
                if not last:
                    nag = layer_setup(b, l + 1)
                elif b + 1 < NSEQ:
                    nag = layer_setup(b + 1, 0)
                else:
                    nag = None
                def tail1(t, last=last, l=l, b=b, nag=nag):
                    if not last:
                        norm_tile(t, False, l + 1, b, nag)
                    else:
                        norm_tile(t, True, L, b, 0)
                        if b + 1 < NSEQ:
                            xload(b + 1, t)

                def tail2(t, last=last, b=b, nag=nag):
                    if last and b + 1 < NSEQ:
                        norm_tile(t, False, 0, b + 1, nag)

                for t in range(NT):
                    pump(unit_stages("O", l, 0, t, slots, 0))
                    flush_loads(n=3)
                    if t == 0:
                        drain()
                    if t >= 1:
                        tail1(t - 1)
                    if t >= 2:
                        tail2(t - 2)
                if last and b + 1 == NSEQ:
                    tail1(NT - 1)
                else:
                    deferred.append(lambda t1=tail1: t1(NT - 1))
                    if NT >= 2:
                        deferred.append(lambda t2=tail2: t2(NT - 2))
                    deferred.append(lambda t2=tail2: t2(NT - 1))
                gi += 1
                ag = nag
        sc.q["sp"].append(([(s_, s_.val) for s_ in out_sem if s_.val > 0], None, None))

        def semh(k):
            return k.h if isinstance(k, DSem) else esem[k]

        def replay(name, e):
            for waits, fn, me in sc.q[name]:
                for k, v in waits:
                    e.wait_ge(semh(k), v)
                if fn is None:
                    continue
                ins = fn(e)
                if isinstance(me[0], DSem):
                    ins.then_inc(me[0].h, 16)
                else:
                    ins.then_inc(esem[me[0]], 1)

        with nc.Block() as block:
            @block.tensor
            def _(e):
                replay("pe", e)

            @block.scalar
            def _(e):
                replay("act", e)

            @block.vector
            def _(e):
                replay("dve", e)

            @block.gpsimd
            def _(e):
                replay("pool", e)

            @block.sync
            def _(e):
                replay("sp", e)
    return nc


def _chunk_cols(w):
    L, K, C = w.shape
    return np.ascontiguousarray(w.reshape(L, 8, 128, C // 128, 128).transpose(0, 3, 2, 1, 4))


def _feat(v):
    sh = v.shape[:-1]
    a = v.reshape(sh + (8, 128))
    return np.moveaxis(a, -1, 0)


def prep_shared(norm_gain, w_mod, b_mod, w_in, w_out, conv_a_w, sgu_w, sgu_b, lru_conv_w, lru_conv_b,
                lru_wa, lru_ba, lru_wx, lru_bx, lru_lambda, final_gain):
    L = w_in.shape[0]
    f32 = np.float32
    sm = np.zeros((128, L, PL), f32)
    sm[:, :, 0:8] = _feat(norm_gain)
    sm[:, :, 8:32] = np.moveaxis(b_mod.reshape(L, 24, 128), -1, 0)
    sm[:, :, 32:56] = _feat(conv_a_w).reshape(128, L, 24)
    sm[:, :, 56:88] = _feat(lru_conv_w).reshape(128, L, 32)
    sm[:, :, 88:96] = _feat(lru_conv_b)
    sm[:, :, 96:104] = _feat(lru_ba.reshape(L, 1024))
    sm[:, :, 104:112] = _feat(lru_bx.reshape(L, 1024))
    sm[:, :, 112:120] = _feat(lru_lambda)
    shared = {
        "small_layers": sm.reshape(128, L * PL),
        "fgain": np.ascontiguousarray(_feat(final_gain)),
        "win": _chunk_cols(w_in),
        "wout": _chunk_cols(w_out),
        "wmod": _chunk_cols(w_mod),
        "wg": np.ascontiguousarray(
            np.stack([lru_wa, lru_wx], axis=0).reshape(2, L, 8, 2, 64, 64).transpose(1, 3, 4, 2, 0, 5)
        ).reshape(L, 128, 8, 128),
        "wst": np.ascontiguousarray(sgu_w.transpose(0, 3, 1, 2)),
        "sgub": np.ascontiguousarray(sgu_b.reshape(L, 1, 8, 128)),
    }
    return shared


def make_in_maps(x, c, shared, n_cores, NSEQ):
    in_maps = []
    B, S, _ = x.shape
    for i in range(n_cores):
        xs = x[i * NSEQ:(i + 1) * NSEQ]
        xT = np.ascontiguousarray(xs.reshape(NSEQ, S, 8, 128).transpose(0, 3, 2, 1))
        cs = c[i * NSEQ:(i + 1) * NSEQ]
        cT = cs.reshape(NSEQ, 8, 128).transpose(2, 1, 0).reshape(128, 8 * NSEQ)
        small = np.ascontiguousarray(
            np.concatenate([shared["small_layers"], shared["fgain"], cT], axis=1).astype(np.float32))
        m = {"xT": xT, "small": small}
        for k in ("win", "wout", "wmod", "wg", "wst", "sgub"):
            m[k] = shared[k]
        in_maps.append(m)
    return in_maps


_NC_CACHE = {}


def kernel(x, c, norm_gain, w_mod, b_mod, w_in, w_out, conv_a_w, sgu_w, sgu_b, lru_conv_w, lru_conv_b,
           lru_wa, lru_ba, lru_wx, lru_bx, lru_lambda, final_gain):
    args = [np.asarray(a, dtype=np.float32) for a in
            (norm_gain, w_mod, b_mod, w_in, w_out, conv_a_w, sgu_w, sgu_b, lru_conv_w, lru_conv_b,
             lru_wa, lru_ba, lru_wx, lru_bx, lru_lambda, final_gain)]
    x = np.asarray(x, dtype=np.float32)
    c = np.asarray(c, dtype=np.float32)
    B, S, _ = x.shape
    L = args[3].shape[0]
    n_cores = 8
    NSEQ = B // n_cores
    shared = prep_shared(*args)
    in_maps = make_in_maps(x, c, shared, n_cores, NSEQ)
    key = (NSEQ, S, L)
    if key not in _NC_CACHE:
        _NC_CACHE[key] = build(NSEQ, S, L)
    nc = _NC_CACHE[key]
    res = run_bass_kernel_spmd(nc, in_maps, core_ids=list(range(n_cores)))
    out = np.empty((B, S, D), np.float32)
    for i in range(n_cores):
        oT = np.asarray(res.results[i]["outT"]).reshape(NSEQ, 128, 8, S)
        out[i * NSEQ:(i + 1) * NSEQ] = oT.transpose(0, 3, 2, 1).reshape(NSEQ, S, D)
    return out
```

```python
import numpy as np
from contextlib import ExitStack
import concourse.bass as bass
import concourse.mybir as mybir
from concourse.bass_utils import run_bass_kernel_spmd

F32 = mybir.dt.float32
BF16 = mybir.dt.bfloat16
AF = mybir.ActivationFunctionType
ALU = mybir.AluOpType

D = 1024
KC = 8
TT = 512
EPS = 1e-6
PL = 120
CL = 32
NW = 18
NSTG = 2
NTMP = 12
B_AX, B_AB, B_AC, B_AZ, B_SU, B_SV, B_SZ, B_RX, B_RZ, B_GA, B_GS, B_GR = range(12)


class DSem:
    def __init__(self, h):
        self.h = h
        self.val = 0


class Buf:
    __slots__ = ("name", "w", "r", "excl")

    def __init__(self, name, excl=False):
        self.name = name
        self.w = None
        self.r = {}
        self.excl = excl


class Sched:
    ENGS = ("pe", "act", "dve", "pool", "sp")

    def __init__(self):
        self.q = {e: [] for e in self.ENGS}
        self.cnt = {e: 0 for e in self.ENGS}
        self.seen = {e: {} for e in self.ENGS}

    def op(self, eng, fn, reads=(), writes=(), dma=None):
        deps = {}

        def add(k, v):
            if eng == "pe" and k == "pe":
                return
            if deps.get(k, 0) < v:
                deps[k] = v

        for b in reads:
            if b.w is not None:
                add(*b.w)
            if b.excl:
                for k, v in b.r.items():
                    if k != eng:
                        add(k, v)
        for b in writes:
            if b.w is not None:
                add(*b.w)
            for k, v in b.r.items():
                add(k, v)
        waits = []
        seen = self.seen[eng]
        for k, v in deps.items():
            if seen.get(k, 0) < v:
                seen[k] = v
                waits.append((k, v))
        if dma is None:
            self.cnt[eng] += 1
            me = (eng, self.cnt[eng])
        else:
            dma.val += 16
            me = (dma, dma.val)
        self.q[eng].append((waits, fn, me))
        for b in reads:
            if b.r.get(me[0], 0) < me[1]:
                b.r[me[0]] = me[1]
        for b in writes:
            b.w = me
            b.r = {}
        return me


def build(NSEQ=2, S=2048, L=4):
    NT = S // TT
    NCH = S // 128
    NSMALL = L * PL + 8 + 8 * NSEQ
    nc = bass.Bass("TRN2", target_bir_lowering=False, dynamic_dma_scratch_size=2048)

    def dram(name, shape, kind="ExternalInput"):
        return nc.dram_tensor(name, shape, F32, kind=kind).ap()

    xT_d = dram("xT", [NSEQ, 128, 8, S])
    small_d = dram("small", [128, NSMALL])
    win_d = dram("win", [L, 96, 128, 8, 128])
    wout_d = dram("wout", [L, 8, 128, 8, 128])
    wmod_d = dram("wmod", [L, 24, 128, 8, 128])
    wg_d = dram("wg", [L, 128, 8, 128])
    wst_d = dram("wst", [L, 128, 8, 128])
    sgub_d = dram("sgub", [L, 1, 8, 128])
    out_d = dram("outT", [NSEQ, 128, 8, S], kind="ExternalOutput")

    sc = Sched()
    es = ExitStack()
    with es:
        def sb(name, shape, dt):
            return es.enter_context(nc.sbuf_tensor(name, shape, dt))

        def mksem(name):
            return DSem(es.enter_context(nc.semaphore(name)))

        esem = {e: es.enter_context(nc.semaphore("s_" + e)) for e in ("pe", "act", "dve", "pool")}

        xT = sb("xT_sb", [128, 8, S], F32)
        hT = sb("hT_sb", [128, 8, S], BF16)
        mT = sb("mT_sb", [128, 8, S], BF16)
        ring = sb("ring", [128, NW, 8, 128], BF16)
        stage = sb("stage", [128, NSTG, 8, 128], F32)
        tmps = sb("tmps", [128, NTMP, TT], F32)
        cv = sb("cv", [128, 3 + S], BF16)
        xcb = sb("xcb", [128, 2, TT], BF16)
        vn = sb("vn", [128, 1, NCH, 128], BF16)
        gbf = sb("gbf", [128, 8, 128], BF16)
        wstb = sb("wstb", [128, 8, 128], BF16)
        brow = sb("brow", [128, 8, 128], BF16)
        small = sb("small_sb", [128, NSMALL], F32)
        coef = sb("coef", [128, L * CL], F32)
        coefAG = sb("coefAG", [128, 2, 16], F32)
        modT = sb("modT", [128, L, 24, NSEQ], F32)
        cact = sb("cact", [128, 8, NSEQ], BF16)
        ident = sb("ident", [128, 128], F32)
        ones_bf = sb("ones_bf", [128, 128], BF16)
        onesrow = sb("onesrow", [128, 128], BF16)
        diagA = sb("diagA", [128, 2, 3, 128], BF16)
        diagR = sb("diagR", [128, 2, 4, 128], BF16)
        bd = sb("bd", [128, 2, 2, 128], BF16)
        stats = sb("stats", [128, NCH, 6], F32)
        mv = sb("mv", [128, NCH, 2], F32)
        lnr = sb("lnr", [128, 2, NCH], F32)
        ltmp = sb("ltmp", [128, 8], F32)
        banks = [es.enter_context(nc.psum_tensor("ps%d" % i, [128, TT], F32)) for i in range(8)]

        xb = [Buf("x%d" % t) for t in range(NT)]
        hb = [Buf("h%d" % t) for t in range(NT)]
        mb = [[Buf("m%d_%d" % (k, t)) for t in range(NT)] for k in range(KC)]
        ringb = [Buf("ring%d" % i) for i in range(NW)]
        stageb = [Buf("stage%d" % i) for i in range(NSTG)]
        tmpb = [Buf("tmp%d" % i) for i in range(NTMP)]
        cvb = [Buf("cv%d" % t) for t in range(NT)]
        cvhalo = Buf("cvhalo")
        xcbb = [Buf("xcb0"), Buf("xcb1")]
        vnb = [Buf("vn0"), Buf("vn0b")]
        vnb[1] = vnb[0]
        gbfb, wstbb, browb, smallb = Buf("gbf"), Buf("wstb"), Buf("brow"), Buf("small")
        coefb, modb, cactb, constb = Buf("coef"), Buf("mod"), Buf("cact"), Buf("const")
        coefAGb = [Buf("cag0"), Buf("cag1")]
        diagAb = [Buf("dA0"), Buf("dA1")]
        diagRb = [Buf("dR0"), Buf("dR1")]
        bdb = [Buf("bd0"), Buf("bd1")]
        statb = Buf("stats")
        bankb = [Buf("bank%d" % i, excl=True) for i in range(8)]

        stage_sem = [mksem("stg%d" % i) for i in range(NSTG)]
        x_sem = [mksem("xs%d" % t) for t in range(NT)]
        NOUT = max(1, (8 * S // 2) // (8 * TT))
        out_sem = [mksem("os%d" % i) for i in range(2 * NT)]
        outb = [Buf("outst%d" % i) for i in range(NOUT)]
        small_sem = mksem("smalls")
        ring_sem = [mksem("rg%d" % i) for i in range(NW)]
        outst = mT[:].bitcast(F32)

        st = {"bank": 0, "chunk": 0}
        free_tmps = list(range(NTMP))

        def talloc():
            assert free_tmps, "out of temporaries"
            i = free_tmps.pop(0)
            return i

        def tfree(i):
            free_tmps.append(i)

        def T(i):
            return tmps[:, i, :]

        def nbank():
            i = st["bank"]
            st["bank"] = (i + 1) % 8
            return i

        def col(c):
            return small[:, c:c + 1]

        def tsl(t):
            return slice(t * TT, (t + 1) * TT)

        def act(out, in_, func, reads, writes, bias=None, scale=None):
            kw = {}
            if bias is not None:
                kw["bias"] = bias
            if scale is not None:
                kw["scale"] = scale
            sc.op("act", lambda e: e.activation(out, in_, func, **kw), reads, writes)

        def tt(eng, out, a, b, op, reads, writes):
            sc.op(eng, lambda e: e.tensor_tensor(out, a, b, op), reads, writes)

        def stt(out, in0, scalar, in1, op0, op1, reads, writes):
            sc.op("dve", lambda e: e.scalar_tensor_tensor(out, in0, scalar, in1, op0, op1), reads, writes)

        def tsc(eng, out, in0, s1, s2, op0, op1, reads, writes):
            if s2 is None:
                sc.op(eng, lambda e: e.tensor_scalar(out, in0, s1, None, op0), reads, writes)
            else:
                sc.op(eng, lambda e: e.tensor_scalar(out, in0, s1, s2, op0, op1), reads, writes)

        def mm(out, lhsT, rhs, start, stop, reads, writes):
            sc.op("pe", lambda e: e.matmul(out, lhsT, rhs, start=start, stop=stop), reads, writes)

        def load_chunk(kind, src, extra=None):
            i = st["chunk"]
            st["chunk"] += 1
            s = i % NSTG
            if kind == "sgub":
                sc.op("sp", lambda e: e.dma_start(out=stage[0:1, s], in_=src), (), [stageb[s]], dma=stage_sem[s])
                sc.op("sp", lambda e: e.dma_start(out=stage[32:33, s], in_=src), (), [stageb[s]], dma=stage_sem[s])
            else:
                sc.op("sp", lambda e: e.dma_start(out=stage[:, s], in_=src), (), [stageb[s]], dma=stage_sem[s])
            return s

        ring_pos = {"i": 0}

        load_q = []

        def load_weight(src, gi=0):
            r = ring_pos["i"] % NW
            ring_pos["i"] += 1
            load_q.append((gi, lambda: sc.op("pool", lambda e: e.dma_start(out=ring[:, r], in_=src), (),
                                             [ringb[r]], dma=ring_sem[r])))
            return r

        def flush_loads(upto_gi=None, n=None):
            while load_q:
                if upto_gi is not None:
                    if load_q[0][0] > upto_gi:
                        break
                elif n is not None:
                    if n <= 0:
                        break
                    n -= 1
                load_q.pop(0)[1]()

        sc.op("sp", lambda e: e.dma_start(out=small[:], in_=small_d), (), [smallb], dma=small_sem)
        sc.op("pool", lambda e: e.memset(ident[:], 1.0), (), [constb])
        sc.op("pool", lambda e: e.affine_select(ident[:], ident[:], [[1, 128]], ALU.is_equal, 0.0,
                                                base=0, channel_multiplier=-1), [constb], [constb])
        sc.op("pool", lambda e: e.memset(ones_bf[:], 1.0), (), [constb])
        sc.op("pool", lambda e: e.memset(onesrow[:], 0.0), (), [constb])
        sc.op("pool", lambda e: e.memset(onesrow[0:1, :], 1.0), [constb], [constb])
        sc.op("pool", lambda e: e.memset(onesrow[32:33, :], 1.0), [constb], [constb])
        sc.op("pool", lambda e: e.memset(brow[:], 0.0), (), [browb])
        sc.op("pool", lambda e: e.memset(bd[:], 0.0), (), [bdb[0], bdb[1]])
        sc.op("pool", lambda e: e.memset(cv[:, 0:3], 0.0), (), [cvhalo])
        cbase = L * PL + 8
        act(cact[:], small[:, cbase:cbase + 8 * NSEQ].rearrange("p (k b) -> p k b", b=NSEQ), AF.Silu,
            [smallb], [cactb])
        for l in range(L):
            b0 = l * PL
            c0 = l * CL
            tsc("dve", coef[:, c0:c0 + 8], small[:, b0 + 96:b0 + 104], 0.5, None, ALU.mult, None, [smallb], [coefb])
            tsc("dve", coef[:, c0 + 8:c0 + 16], small[:, b0 + 104:b0 + 112], 0.5, None, ALU.mult, None,
                [smallb], [coefb])
            tsc("dve", coef[:, c0 + 16:c0 + 24], small[:, b0 + 88:b0 + 96], 0.5, None, ALU.mult, None,
                [smallb], [coefb])
            act(ltmp[:], small[:, b0 + 112:b0 + 120], AF.Exp, [smallb], [statb], scale=-1.0)
            act(ltmp[:], ltmp[:], AF.Ln, [statb], [statb], bias=1.0)
            tsc("dve", coef[:, c0 + 24:c0 + 32], ltmp[:], -4.0, None, ALU.mult, None, [statb], [coefb])

        def grp_weights(kind, l, j):
            if kind == "A":
                blks = [B_AX, B_AC, B_AB, B_AZ, B_GA, B_SV]
            elif kind == "R":
                blks = [B_RX, B_RZ, B_GR]
            elif kind == "S":
                blks = [B_SU, B_SZ, B_GS]
            else:
                return [wout_d[l, j]]
            return [win_d[l, b * 8 + j] for b in blks]

        def main_mm(bk, r, t):
            for k in range(KC):
                mm(banks[bk][:], ring[:, r, k, :], hT[:, k, tsl(t)], k == 0, k == KC - 1,
                   [ringb[r], hb[t]], [bankb[bk]])

        def gate_pair(bz, bg, raw):
            t1, t2 = talloc(), talloc()
            act(T(t1), banks[bz][:], AF.Tanh, [bankb[bz]], [tmpb[t1]], scale=0.5)
            act(T(t2), banks[bg][:], AF.Tanh, [bankb[bg]], [tmpb[t2]], scale=0.5)
            stt(T(t1), T(t1), 1.0, banks[bz][:], ALU.add, ALU.mult, [tmpb[t1], bankb[bz]], [tmpb[t1]])
            stt(T(t1), T(t2), 1.0, T(t1), ALU.add, ALU.mult, [tmpb[t1], tmpb[t2]], [tmpb[t1]])
            tfree(t2)
            return t1

        def unit_stages(kind, l, j, t, slots, par):
            b0 = l * PL
            c0 = l * CL
            u = {}
            if kind == "A":
                def s0():
                    bx, bc, bb_, bz, bg = (nbank() for _ in range(5))
                    for bk, r in zip((bx, bc, bb_, bz, bg), slots):
                        main_mm(bk, r, t)
                    t1 = talloc()
                    act(T(t1), banks[bx][:], AF.Copy, [bankb[bx]], [tmpb[t1]])
                    tt("dve", cv[:, 3 + t * TT:3 + (t + 1) * TT], banks[bc][:], T(t1), ALU.mult,
                       [bankb[bc], tmpb[t1]], [cvb[t]])
                    tfree(t1)
                    q = gate_pair(bz, bg, None)
                    tt("dve", T(q), banks[bb_][:], T(q), ALU.mult, [bankb[bb_], tmpb[q]], [tmpb[q]])
                    u["q"] = q
                    rv = slots[5]
                    bvv = nbank()
                    for c4 in range(4):
                        c = t * 4 + c4
                        for k in range(KC):
                            mm(banks[bvv][:, c4 * 128:(c4 + 1) * 128], hT[:, k, c * 128:(c + 1) * 128],
                               ring[:, rv, k, :], k == 0, k == KC - 1, [hb[t], ringb[rv]], [bankb[bvv]])
                    tv = talloc()
                    act(T(tv), banks[bvv][:], AF.Copy, [bankb[bvv]], [tmpb[tv]])
                    for c4 in range(4):
                        c = t * 4 + c4
                        src = T(tv)[:, c4 * 128:(c4 + 1) * 128]
                        sc.op("dve", lambda e, c=c, src=src: e.bn_stats(stats[:, c, :], src), [tmpb[tv]], [statb])
                        sc.op("dve", lambda e, c=c: e.bn_aggr(mv[:, c, :], stats[:, c, :]), [statb], [statb])
                    c0_ = t * 4
                    act(lnr[:, 0, c0_:c0_ + 4], mv[:, c0_:c0_ + 4, 1], AF.Sqrt, [statb], [lnrb], bias=EPS)
                    sc.op("dve", lambda e: e.reciprocal(lnr[:, 0, c0_:c0_ + 4], lnr[:, 0, c0_:c0_ + 4]), [lnrb], [lnrb])
                    stt(lnr[:, 1, c0_:c0_ + 4], mv[:, c0_:c0_ + 4, 0], -1.0, lnr[:, 0, c0_:c0_ + 4], ALU.mult, ALU.mult,
                        [statb, lnrb], [lnrb])
                    for c4 in range(4):
                        c = c0_ + c4
                        tsc("pool", vn[:, 0, c, :], T(tv)[:, c4 * 128:(c4 + 1) * 128],
                            lnr[:, 0, c:c + 1], lnr[:, 1, c:c + 1], ALU.mult, ALU.add,
                            [tmpb[tv], lnrb], [vnb[par]])
                    tfree(tv)

                def s1():
                    bv = nbank()
                    rd = [diagAb[par], cvb[t], cvhalo] + ([cvb[t - 1]] if t > 0 else [])
                    for tap in range(3):
                        mm(banks[bv][:], diagA[:, par, tap, :], cv[:, 1 + t * TT + tap:1 + t * TT + tap + TT],
                           tap == 0, tap == 2, rd, [bankb[bv]])
                    q = u["q"]
                    tt("dve", mT[:, j, tsl(t)], banks[bv][:], T(q), ALU.mult, [bankb[bv], tmpb[q]], [mb[j][t]])
                    tfree(q)
                return [s0, s1]
            if kind == "R":
                def s0():
                    bx, bz, bg = (nbank() for _ in range(3))
                    for bk, r in zip((bx, bz, bg), slots):
                        main_mm(bk, r, t)
                    act(cv[:, 3 + t * TT:3 + (t + 1) * TT], banks[bx][:], AF.Copy, [bankb[bx]], [cvb[t]])
                    u["q"] = gate_pair(bz, bg, None)

                def s1():
                    bv = nbank()
                    rd = [diagRb[par], cvb[t], cvhalo] + ([cvb[t - 1]] if t > 0 else [])
                    for tap in range(4):
                        mm(banks[bv][:], diagR[:, par, tap, :], cv[:, t * TT + tap:t * TT + tap + TT],
                           tap == 0, tap == 3, rd, [bankb[bv]])
                    xp = t % 2
                    act(xcb[:, xp, :], banks[bv][:], AF.Identity, [bankb[bv], smallb], [xcbb[xp]],
                        bias=col(b0 + 88 + j))
                    t3 = talloc()
                    act(T(t3), banks[bv][:], AF.Identity, [bankb[bv], coefb], [tmpb[t3]],
                        bias=coef[:, c0 + 16 + j:c0 + 17 + j], scale=0.5)
                    u["xcf"] = t3

                def s2():
                    xp = t % 2
                    bra, bri = nbank(), nbank()
                    mm(banks[bra][:], bd[:, par, 0, :], xcb[:, xp, :], True, True, [bdb[par], xcbb[xp]], [bankb[bra]])
                    mm(banks[bri][:], bd[:, par, 1, :], xcb[:, xp, :], True, True, [bdb[par], xcbb[xp]], [bankb[bri]])
                    t4, t5, t6 = talloc(), talloc(), talloc()
                    act(T(t4), banks[bra][:], AF.Tanh, [bankb[bra], coefb], [tmpb[t4]],
                        bias=coef[:, c0 + j:c0 + j + 1], scale=0.5)
                    act(T(t5), banks[bri][:], AF.Tanh, [bankb[bri], coefb], [tmpb[t5]],
                        bias=coef[:, c0 + 8 + j:c0 + 9 + j], scale=0.5)
                    hc = coef[:, c0 + 24 + j:c0 + 25 + j]
                    act(T(t4), T(t4), AF.Exp, [tmpb[t4], coefb], [tmpb[t4]], bias=hc, scale=hc)
                    tt("pool", T(t6), T(t4), T(t4), ALU.mult, [tmpb[t4]], [tmpb[t6]])
                    tsc("pool", T(t6), T(t6), 0.9999999, 0.0, ALU.min, ALU.max, [tmpb[t6]], [tmpb[t6]])
                    act(T(t6), T(t6), AF.Sqrt, [tmpb[t6]], [tmpb[t6]], bias=1.0, scale=-1.0)
                    t3 = u["xcf"]
                    stt(T(t5), T(t5), 1.0, T(t3), ALU.add, ALU.mult, [tmpb[t5], tmpb[t3]], [tmpb[t5]])
                    tfree(t3)
                    tt("pool", T(t5), T(t5), T(t6), ALU.mult, [tmpb[t5], tmpb[t6]], [tmpb[t5]])
                    tfree(t6)
                    t7 = talloc()
                    prev = scan_prev.get("t")
                    if t == 0:
                        init = 0.0
                        rd = [tmpb[t4], tmpb[t5]]
                    else:
                        init = T(prev)[:, TT - 1:TT]
                        rd = [tmpb[t4], tmpb[t5], tmpb[prev]]
                    sc.op("dve", lambda e: e.tensor_tensor_scan(T(t7), T(t4), T(t5), init, ALU.mult, ALU.add),
                          rd, [tmpb[t7]])
                    if prev is not None:
                        tfree(prev)
                    tfree(t4)
                    tfree(t5)
                    q = u["q"]
                    tt("pool", T(q), T(t7), T(q), ALU.mult, [tmpb[t7], tmpb[q]], [tmpb[q]])
                    tt("dve", mT[:, j, tsl(t)], mT[:, j, tsl(t)], T(q), ALU.add, [mb[j][t], tmpb[q]], [mb[j][t]])
                    tfree(q)
                    if t == NT - 1:
                        tfree(t7)
                        scan_prev["t"] = None
                    else:
                        scan_prev["t"] = t7
                return [s0, s1, s2]
            if kind == "S":
                def s0():
                    bu, bz, bg = (nbank() for _ in range(3))
                    for bk, r in zip((bu, bz, bg), slots):
                        main_mm(bk, r, t)
                    bzf = nbank()
                    for c4 in range(4):
                        c = t * 4 + c4
                        o = banks[bzf][:, c4 * 128:(c4 + 1) * 128]
                        mm(o, vn[:, 0, c, :], wstb[:, j, :], True, False, [vnb[par], wstbb], [bankb[bzf]])
                        mm(o, onesrow[:], brow[:, j, :], False, True, [constb, browb], [bankb[bzf]])
                    q = gate_pair(bz, bg, None)
                    tt("dve", T(q), banks[bu][:], T(q), ALU.mult, [bankb[bu], tmpb[q]], [tmpb[q]])
                    tt("dve", T(q), banks[bzf][:], T(q), ALU.mult, [bankb[bzf], tmpb[q]], [tmpb[q]])
                    tt("pool", mT[:, j, tsl(t)], mT[:, j, tsl(t)], T(q), ALU.add, [mb[j][t], tmpb[q]], [mb[j][t]])
                    tfree(q)
                return [s0]
            if kind == "V":
                def s0():
                    r = slots[0]
                    vbanks = []
                    for c in range(NCH):
                        if c % 4 == 0:
                            vbanks.append(nbank())
                        bk = vbanks[-1]
                        for k in range(KC):
                            mm(banks[bk][:, (c % 4) * 128:(c % 4 + 1) * 128], hT[:, k, c * 128:(c + 1) * 128],
                               ring[:, r, k, :], k == 0, k == KC - 1, [hb[c // 4], ringb[r]], [bankb[bk]])
                    for c in range(NCH):
                        bk = vbanks[c // 4]
                        src = banks[bk][:, (c % 4) * 128:(c % 4 + 1) * 128]
                        sc.op("dve", lambda e, c=c, src=src: e.bn_stats(stats[:, c, :], src), [bankb[bk]], [statb])
                        sc.op("dve", lambda e, c=c: e.bn_aggr(mv[:, c, :], stats[:, c, :]), [statb], [statb])
                    act(lnr[:, 0, :], mv[:, :, 1], AF.Sqrt, [statb], [statb], bias=EPS)
                    sc.op("dve", lambda e: e.reciprocal(lnr[:, 0, :], lnr[:, 0, :]), [statb], [statb])
                    stt(lnr[:, 1, :], mv[:, :, 0], -1.0, lnr[:, 0, :], ALU.mult, ALU.mult, [statb], [statb])
                    for c in range(NCH):
                        bk = vbanks[c // 4]
                        src = banks[bk][:, (c % 4) * 128:(c % 4 + 1) * 128]
                        act(vn[:, 0, c, :], src, AF.Identity, [bankb[bk], statb], [vnb[par]],
                            bias=lnr[:, 1, c:c + 1], scale=lnr[:, 0, c:c + 1])
                return [s0]
            if kind == "O":
                def s0():
                    gp = u_par["ag"]
                    for o in range(KC):
                        bk = nbank()
                        r = slots[o]
                        for k in range(KC):
                            mm(banks[bk][:], ring[:, r, k, :], mT[:, k, tsl(t)], k == 0, k == KC - 1,
                               [ringb[r], mb[k][t]], [bankb[bk]])
                        stt(xT[:, o, tsl(t)], banks[bk][:], coefAG[:, gp, 8 + o:9 + o], xT[:, o, tsl(t)],
                            ALU.mult, ALU.add, [bankb[bk], coefAGb[gp], xb[t]], [xb[t]])
                return [s0]
            raise ValueError(kind)

        scan_prev = {"t": None}
        vtmps = []
        lnrb = Buf("lnr")
        u_par = {"ag": 0}

        def group_setup(kind, l, j, par):
            b0 = l * PL
            if kind == "A":
                for tap in range(3):
                    tsc("pool", diagA[:, par, tap, :], ident[:], col(b0 + 32 + tap * 8 + j), 0.0, ALU.mult, ALU.add,
                        [constb, smallb], [diagAb[par]])
            elif kind == "R":
                for tap in range(4):
                    tsc("pool", diagR[:, par, tap, :], ident[:], col(b0 + 56 + tap * 8 + j), 0.0, ALU.mult, ALU.add,
                        [constb, smallb], [diagRb[par]])
                gv = gbf[:, j, :].rearrange("p (g e) -> p g e", g=2)
                sc.op("pool", lambda e: e.tensor_copy(bd[0:64, par, :, 0:64], gv[0:64]), [gbfb], [bdb[par]])
                sc.op("pool", lambda e: e.tensor_copy(bd[64:128, par, :, 64:128], gv[64:128]), [gbfb], [bdb[par]])

        def norm_tile(t, final, l, b, ag):
            sc.op("act", lambda e: e.activation(hT[:, :, tsl(t)], xT[:, :, tsl(t)], AF.Square), [xb[t]], [hb[t]])
            bk = nbank()
            for k in range(KC):
                mm(banks[bk][:], ones_bf[:], hT[:, k, tsl(t)], k == 0, k == KC - 1, [constb, hb[t]], [bankb[bk]])
            t1 = talloc()
            act(T(t1), banks[bk][:], AF.Sqrt, [bankb[bk]], [tmpb[t1]], bias=EPS, scale=1.0 / D)
            sc.op("dve", lambda e: e.reciprocal(banks[bk][:], T(t1)), [tmpb[t1]], [bankb[bk]])
            tfree(t1)
            if not final:
                for k in range(KC):
                    t2 = talloc()
                    tt("dve", T(t2), xT[:, k, tsl(t)], banks[bk][:], ALU.mult, [xb[t], bankb[bk]], [tmpb[t2]])
                    act(hT[:, k, tsl(t)], T(t2), AF.Identity, [tmpb[t2], coefAGb[ag], modb], [hb[t]],
                        bias=modT[:, l, k, b:b + 1], scale=coefAG[:, ag, k:k + 1])
                    tfree(t2)
            else:
                fg = L * PL
                H2 = TT // 2
                mview = mT[:, :, tsl(t)].bitcast(F32)
                hview = hT[:, :, tsl(t)].bitcast(F32)
                mbt = [mb[k_][t] for k_ in range(KC)]
                for k in range(KC):
                    stt(mview[:, k, :], xT[:, k, t * TT:t * TT + H2], col(fg + k), banks[bk][:, 0:H2],
                        ALU.mult, ALU.mult, [xb[t], smallb, bankb[bk]], mbt)
                    stt(hview[:, k, :], xT[:, k, t * TT + H2:(t + 1) * TT], col(fg + k), banks[bk][:, H2:TT],
                        ALU.mult, ALU.mult, [xb[t], smallb, bankb[bk]], [hb[t]])
                sc.op("sp", lambda e: e.dma_start(out=out_d[b, :, :, t * TT:t * TT + H2], in_=mview),
                      mbt, (), dma=out_sem[2 * t])
                sc.op("sp", lambda e: e.dma_start(out=out_d[b, :, :, t * TT + H2:(t + 1) * TT], in_=hview),
                      [hb[t]], (), dma=out_sem[2 * t + 1])

        groups = []
        for b in range(NSEQ):
            for l in range(L):
                if b == 0 and l == 0:
                    groups += [("M", 0, 0, q) for q in range(8)]
                for j in range(KC):
                    groups.append(("A", b, l, j))
                    groups.append(("RS", b, l, j))
                    if b == 0 and l + 1 < L:
                        groups.append(("M", 0, l + 1, j))
                groups.append(("O", b, l, 0))
        gslots = {}

        def nchunks(gi):
            if gi >= len(groups):
                return 0
            return {"A": 6, "RS": 6, "O": 8, "M": 3}[groups[gi][0]]

        def emit_loads(gi):
            if gi >= len(groups) or gi in gslots:
                return
            kind, b, l, j = groups[gi]
            if kind == "RS":
                srcs = grp_weights("R", l, j) + grp_weights("S", l, j)
            elif kind == "O":
                srcs = [wout_d[l, o] for o in range(KC)]
            elif kind == "M":
                srcs = [wmod_d[l, 3 * j + fi] for fi in range(3)]
            else:
                srcs = grp_weights(kind, l, j)
            gslots[gi] = [load_weight(src, gi) for src in srcs]

        pending = []

        def pump(new_unit):
            if new_unit is not None:
                pending.insert(0, [new_unit, 0])
            for ent in list(pending):
                stages, i = ent
                stages[i]()
                ent[1] += 1
            pending[:] = [e for e in pending if e[1] < len(e[0])]

        def drain():
            while pending:
                pump(None)

        def layer_setup(b, l):
            ag = (b * L + l) % 2
            stt(coefAG[:, ag, 0:8], modT[:, l, 8:16, b], 1.0, small[:, l * PL:l * PL + 8], ALU.add, ALU.mult,
                [modb, smallb], [coefAGb[ag]])
            tsc("dve", coefAG[:, ag, 8:16], modT[:, l, 16:24, b], 0.25, None, ALU.mult, None,
                [modb], [coefAGb[ag]])
            s = load_chunk("wst", wst_d[l])
            sc.op("pool", lambda e, s=s: e.affine_select(wstb[:], stage[:, s], [[0, 8], [1, 128]], ALU.is_ge, 0.0,
                                                         base=0, channel_multiplier=-1), [stageb[s]], [wstbb])
            s = load_chunk("sgub", sgub_d[l])
            sc.op("dve", lambda e, s=s: e.tensor_copy(brow[0:1], stage[0:1, s]), [stageb[s]], [browb])
            sc.op("dve", lambda e, s=s: e.tensor_copy(brow[32:33], stage[32:33, s]), [stageb[s]], [browb])
            sc.op("dve", lambda e, s=s: e.tensor_tensor(brow[32:33], stage[32:33, s], brow[32:33], ALU.subtract),
                  [stageb[s], browb], [browb])
            s = load_chunk("wg", wg_d[l])
            sc.op("pool", lambda e, s=s: e.tensor_copy(gbf[:], stage[:, s]), [stageb[s]], [gbfb])
            return ag

        def prefetch(gi):
            emit_loads(gi)
            emit_loads(gi + 1)
            if nchunks(gi) + nchunks(gi + 1) + nchunks(gi + 2) <= NW:
                emit_loads(gi + 2)
            flush_loads(upto_gi=gi)


        def run_M(lt, q, slots):
            bk = nbank()
            for fi in range(3):
                for k in range(KC):
                    mm(banks[bk][:, fi * NSEQ:(fi + 1) * NSEQ], ring[:, slots[fi], k, :], cact[:, k, :],
                       k == 0, k == KC - 1, [ringb[slots[fi]], cactb], [bankb[bk]])
            pv = banks[bk][:, 0:3 * NSEQ].rearrange("p (f b) -> p f b", b=NSEQ)
            c0 = lt * PL + 8 + 3 * q
            for b2 in range(NSEQ):
                tt("dve", modT[:, lt, 3 * q:3 * q + 3, b2], pv[:, :, b2], small[:, c0:c0 + 3], ALU.add,
                   [bankb[bk], smallb], [modb])

        def xload(b, t):
            sc.op("sp", lambda e: e.dma_start(out=xT[:, :, tsl(t)], in_=xT_d[b, :, :, tsl(t)]),
                  (), [xb[t]], dma=x_sem[t])

        gi = 0
        emit_loads(0)
        ag = None
        start_norms = []
        for b in range(NSEQ):
            if b == 0:
                for t in range(NT):
                    xload(0, t)
                while groups[gi][0] == "M":
                    prefetch(gi)
                    run_M(groups[gi][2], groups[gi][3], gslots.pop(gi))
                    gi += 1
                ag = layer_setup(0, 0)
                for t in range(min(2, NT)):
                    norm_tile(t, False, 0, 0, ag)
                start_norms = list(range(min(2, NT), NT))
            for l in range(L):
                while groups[gi][0] != "O":
                    kind, gb, gl, j = groups[gi]
                    prefetch(gi)
                    slots = gslots.pop(gi)
                    par = j % 2
                    if kind == "M":
                        run_M(gl, j, slots)
                    elif kind == "A":
                        assert (gb, gl) == (b, l)
                        group_setup("A", l, j, par)
                        for t in range(NT):
                            if t == min(1, NT - 1):
                                group_setup("R", l, j, par)
                            pump(unit_stages("A", l, j, t, slots, par))
                            flush_loads(n=3)
                            if start_norms:
                                norm_tile(start_norms.pop(0), False, 0, 0, ag)
                    else:
                        assert (gb, gl) == (b, l)
                        for t in range(NT):
                            pump(unit_stages("R", l, j, t, slots[:3], par))
                            flush_loads(n=1)
                            pump(unit_stages("S", l, j, t, slots[3:], par))
                            flush_loads(n=2)
                    gi += 1
                prefetch(gi)
                slots = gslots.pop(gi)
                u_par["ag"] = ag
                last = l + 1 == L
                if not last:
                    nag = layer_setup(b, l + 1)
                elif b + 1 < NSEQ:
                    nag = layer_setup(b + 1, 0)
                else:
                    nag = None
                def tail1(t):
                    if not last:
                        norm_tile(t, False, l + 1, b, nag)
                    else:
                        norm_tile(t, True, L, b, 0)
                        if b + 1 < NSEQ:
                            xload(b + 1, t)

                def tail2(t):
                    if last and b + 1 < NSEQ:
                        norm_tile(t, False, 0, b + 1, nag)

                for t in range(NT):
                    pump(unit_stages("O", l, 0, t, slots, 0))
                    flush_loads(n=3)
                    if t == 0:
                        drain()
                    if t >= 1:
                        tail1(t - 1)
                    if t >= 2:
                        tail2(t - 2)
                tail1(NT - 1)
                if NT >= 2:
                    tail2(NT - 2)
                tail2(NT - 1)
                gi += 1
                ag = nag
        sc.q["sp"].append(([(s_, s_.val) for s_ in out_sem if s_.val > 0], None, None))

        def semh(k):
            return k.h if isinstance(k, DSem) else esem[k]

        def replay(name, e):
            for waits, fn, me in sc.q[name]:
                for k, v in waits:
                    e.wait_ge(semh(k), v)
                if fn is None:
                    continue
                ins = fn(e)
                if isinstance(me[0], DSem):
                    ins.then_inc(me[0].h, 16)
                else:
                    ins.then_inc(esem[me[0]], 1)

        with nc.Block() as block:
            @block.tensor
            def _(e):
                replay("pe", e)

            @block.scalar
            def _(e):
                replay("act", e)

            @block.vector
            def _(e):
                replay("dve", e)

            @block.gpsimd
            def _(e):
                replay("pool", e)

            @block.sync
            def _(e):
                replay("sp", e)
    return nc


def _chunk_cols(w):
    L, K, C = w.shape
    return np.ascontiguousarray(w.reshape(L, 8, 128, C // 128, 128).transpose(0, 3, 2, 1, 4))


def _feat(v):
    sh = v.shape[:-1]
    a = v.reshape(sh + (8, 128))
    return np.moveaxis(a, -1, 0)


def prep_shared(norm_gain, w_mod, b_mod, w_in, w_out, conv_a_w, sgu_w, sgu_b, lru_conv_w, lru_conv_b,
                lru_wa, lru_ba, lru_wx, lru_bx, lru_lambda, final_gain):
    L = w_in.shape[0]
    f32 = np.float32
    sm = np.zeros((128, L, PL), f32)
    sm[:, :, 0:8] = _feat(norm_gain)
    sm[:, :, 8:32] = np.moveaxis(b_mod.reshape(L, 24, 128), -1, 0)
    sm[:, :, 32:56] = _feat(conv_a_w).reshape(128, L, 24)
    sm[:, :, 56:88] = _feat(lru_conv_w).reshape(128, L, 32)
    sm[:, :, 88:96] = _feat(lru_conv_b)
    sm[:, :, 96:104] = _feat(lru_ba.reshape(L, 1024))
    sm[:, :, 104:112] = _feat(lru_bx.reshape(L, 1024))
    sm[:, :, 112:120] = _feat(lru_lambda)
    shared = {
        "small_layers": sm.reshape(128, L * PL),
        "fgain": np.ascontiguousarray(_feat(final_gain)),
        "win": _chunk_cols(w_in),
        "wout": _chunk_cols(w_out),
        "wmod": _chunk_cols(w_mod),
        "wg": np.ascontiguousarray(
            np.stack([lru_wa, lru_wx], axis=0).reshape(2, L, 8, 2, 64, 64).transpose(1, 3, 4, 2, 0, 5)
        ).reshape(L, 128, 8, 128),
        "wst": np.ascontiguousarray(sgu_w.transpose(0, 3, 1, 2)),
        "sgub": np.ascontiguousarray(sgu_b.reshape(L, 1, 8, 128)),
    }
    return shared


def make_in_maps(x, c, shared, n_cores, NSEQ):
    in_maps = []
    B, S, _ = x.shape
    for i in range(n_cores):
        xs = x[i * NSEQ:(i + 1) * NSEQ]
        xT = np.ascontiguousarray(xs.reshape(NSEQ, S, 8, 128).transpose(0, 3, 2, 1))
        cs = c[i * NSEQ:(i + 1) * NSEQ]
        cT = cs.reshape(NSEQ, 8, 128).transpose(2, 1, 0).reshape(128, 8 * NSEQ)
        small = np.ascontiguousarray(
            np.concatenate([shared["small_layers"], shared["fgain"], cT], axis=1).astype(np.float32))
        m = {"xT": xT, "small": small}
        for k in ("win", "wout", "wmod", "wg", "wst", "sgub"):
            m[k] = shared[k]
        in_maps.append(m)
    return in_maps


_NC_CACHE = {}


def kernel(x, c, norm_gain, w_mod, b_mod, w_in, w_out, conv_a_w, sgu_w, sgu_b, lru_conv_w, lru_conv_b,
           lru_wa, lru_ba, lru_wx, lru_bx, lru_lambda, final_gain):
    args = [np.asarray(a, dtype=np.float32) for a in
            (norm_gain, w_mod, b_mod, w_in, w_out, conv_a_w, sgu_w, sgu_b, lru_conv_w, lru_conv_b,
             lru_wa, lru_ba, lru_wx, lru_bx, lru_lambda, final_gain)]
    x = np.asarray(x, dtype=np.float32)
    c = np.asarray(c, dtype=np.float32)
    B, S, _ = x.shape
    L = args[3].shape[0]
    n_cores = 8
    NSEQ = B // n_cores
    shared = prep_shared(*args)
    in_maps = make_in_maps(x, c, shared, n_cores, NSEQ)
    key = (NSEQ, S, L)
    if key not in _NC_CACHE:
        _NC_CACHE[key] = build(NSEQ, S, L)
    nc = _NC_CACHE[key]
    res = run_bass_kernel_spmd(nc, in_maps, core_ids=list(range(n_cores)))
    out = np.empty((B, S, D), np.float32)
    for i in range(n_cores):
        oT = np.asarray(res.results[i]["outT"]).reshape(NSEQ, 128, 8, S)
        out[i * NSEQ:(i + 1) * NSEQ] = oT.transpose(0, 3, 2, 1).reshape(NSEQ, S, D)
    return out
```
